# Optimizing a Trainium2 kernel written in Bass

```python
import math
import jax, jax.numpy as jnp
from jax import lax
import numpy as np

D_MODEL = 2048
BATCH = 4
SEQ = 2048
DEPTH = 4
DEC_BATCH = 8
DEC_SEQ = 8
PAST_LEN = 16384
PAGE_SIZE = 128

N_A_LAYERS = DEPTH // 2
N_B_LAYERS = DEPTH - N_A_LAYERS
HEAD_DIM = 128
N_HEADS = D_MODEL // HEAD_DIM
A_KV_HEADS = 4
A_GROUP = N_HEADS // A_KV_HEADS
IDX_HEADS = 16
IDX_DIM = 64
TOPK_MAX = 256
B_GROUPS = ((128, 1), (512, 4), (2048, 16))
N_B_GROUPS = len(B_GROUPS)
B_KV_HEADS = 4
B_GROUP = N_HEADS // B_KV_HEADS
B_WINDOW_MAX = max(w for w, _ in B_GROUPS)
D_FF = ((8 * D_MODEL // 3 + 127) // 128) * 128
CONV_WIDTH = 3
NUM_BUCKETS = 32
T5_MAX_DISTANCE = 2048
Q_BLOCK = 128
LN_EPS = 1e-5
ALPHA = (2 * DEPTH) ** 0.25
BETA = (8 * DEPTH) ** -0.25
NEG = -1e30

A_Q = N_HEADS * HEAD_DIM
A_KV = A_KV_HEADS * HEAD_DIM
A_QI = IDX_HEADS * IDX_DIM
A_SPLITS = (A_Q, A_Q + A_KV, A_Q + 2 * A_KV, A_Q + 2 * A_KV + A_QI, A_Q + 2 * A_KV + A_QI + IDX_DIM)
A_IN = A_SPLITS[-1] + IDX_HEADS
B_Q = N_B_GROUPS * N_HEADS * HEAD_DIM
B_KV = B_KV_HEADS * HEAD_DIM

kernel_name = 'yoco_dsa_dilated_convffn_step'


def layer_norm(x, g, b):
    x32 = x.astype(jnp.float32)
    mu = x32.mean(-1, keepdims=True)
    var = jnp.square(x32 - mu).mean(-1, keepdims=True)
    return ((x32 - mu) * lax.rsqrt(var + LN_EPS) * g + b).astype(x.dtype)


def t5_bucket(dist):
    dist = jnp.maximum(dist, 0)
    exact = NUM_BUCKETS // 2
    far = exact + (jnp.log(jnp.maximum(dist, 1).astype(jnp.float32) / exact)
                   / math.log(T5_MAX_DISTANCE / exact) * (NUM_BUCKETS - exact)).astype(jnp.int32)
    return jnp.where(dist < exact, dist, jnp.minimum(far, NUM_BUCKETS - 1))


def gather_rows(x, idx):
    return jax.vmap(lambda xb, ib: xb[ib])(x, idx)


def over_query_blocks(fn, arrays, qpos):
    T = qpos.shape[0]
    if T % Q_BLOCK:
        return fn(*arrays, qpos)
    nb = T // Q_BLOCK
    blocks = tuple(jnp.moveaxis(a.reshape(a.shape[0], nb, Q_BLOCK, *a.shape[2:]), 1, 0) for a in arrays)
    out = lax.map(lambda args: fn(*args), blocks + (qpos.reshape(nb, Q_BLOCK),))
    out = jnp.moveaxis(out, 0, 1)
    return out.reshape(out.shape[0], T, *out.shape[3:])


def conv_ffn(h, state, w_up, conv_w, conv_b, w_down):
    T = h.shape[1]
    gate, up = jnp.split(h @ w_up, 2, axis=-1)
    ext = jnp.concatenate([state.astype(gate.dtype), gate], axis=1)
    conv = conv_b + sum(ext[:, j:j + T] * conv_w[j] for j in range(CONV_WIDTH))
    return (jax.nn.silu(conv) * up) @ w_down, ext[:, ext.shape[1] - (CONV_WIDTH - 1):]


def a_project(h, w_in, kn_g, kn_b):
    B, T = h.shape[:2]
    q, k, v, qi, ki, wi = jnp.split(h @ w_in, A_SPLITS, axis=-1)
    return (q.reshape(B, T, A_KV_HEADS, A_GROUP, HEAD_DIM),
            k.reshape(B, T, A_KV_HEADS, HEAD_DIM),
            v.reshape(B, T, A_KV_HEADS, HEAD_DIM),
            qi.reshape(B, T, IDX_HEADS, IDX_DIM),
            layer_norm(ki, kn_g, kn_b),
            wi * IDX_HEADS ** -0.5)


def indexer_scores(qi, wi, ki):
    s = jnp.einsum('bqhd,bsd->bqhs', qi, ki, preferred_element_type=jnp.float32) * IDX_DIM ** -0.5
    return jnp.einsum('bqhs,bqh->bqs', jax.nn.relu(s), wi.astype(jnp.float32))


def attend_selected(q, k_sel, v_sel, valid, dist, rel_bias):
    G, R = q.shape[2], q.shape[3]
    logits = jnp.einsum('bqgrd,bqkgd->bqgrk', q, k_sel, preferred_element_type=jnp.float32) * HEAD_DIM ** -0.5
    bias = jnp.moveaxis(rel_bias[t5_bucket(dist)].reshape(*dist.shape, G, R), 2, -1)
    logits = jnp.where(valid[:, :, None, None, :], logits + bias, NEG)
    p = jax.nn.softmax(logits, axis=-1)
    return jnp.einsum('bqgrk,bqkgd->bqgrd', p.astype(v_sel.dtype), v_sel)


def mixer_a_prompt(h, w_in, w_o, kn_g, kn_b, rel_bias):
    B, T = h.shape[:2]
    q, k, v, qi, ki, wi = a_project(h, w_in, kn_g, kn_b)
    topk = min(TOPK_MAX, T // 4)
    key_pos = jnp.arange(T)

    def block(qb, qib, wib, qpos):
        sc = indexer_scores(qib, wib, ki)
        sc = jnp.where(key_pos[None, None, :] <= qpos[None, :, None], sc, -jnp.inf)
        _, idx = lax.top_k(sc, topk)
        return attend_selected(qb, gather_rows(k, idx), gather_rows(v, idx),
                               idx <= qpos[None, :, None], qpos[None, :, None] - idx, rel_bias)

    o = over_query_blocks(block, (q, qi, wi), jnp.arange(T))
    return o.reshape(B, T, -1) @ w_o, k, v, ki


def mixer_a_sample(h, w_in, w_o, kn_g, kn_b, ck, cv, cki, page_table, rel_bias):
    DB, T = h.shape[:2]
    page = ck.shape[1]
    past = page_table.shape[1] * page
    q, k, v, qi, ki, wi = a_project(h, w_in, kn_g, kn_b)
    topk = min(TOPK_MAX, (past + T) // 4)
    keys_idx = jnp.concatenate([cki[page_table].reshape(DB, past, IDX_DIM).astype(ki.dtype), ki], axis=1)
    qpos = past + jnp.arange(T)
    sc = indexer_scores(qi, wi, keys_idx)
    sc = jnp.where(jnp.arange(past + T)[None, None, :] <= qpos[None, :, None], sc, -jnp.inf)
    _, idx = lax.top_k(sc, topk)
    in_past = idx < past
    pidx = jnp.minimum(idx, past - 1)
    phys = page_table[jnp.arange(DB)[:, None, None], pidx // page] * page + pidx % page
    nidx = jnp.clip(idx - past, 0, T - 1)

    def pick(cache, new):
        old = cache.reshape(-1, *cache.shape[2:])[phys]
        return jnp.where(in_past[..., None, None], old.astype(new.dtype), gather_rows(new, nidx))

    o = attend_selected(q, pick(ck, k), pick(cv, v), idx <= qpos[None, :, None],
                        qpos[None, :, None] - idx, rel_bias)
    return o.reshape(DB, T, -1) @ w_o, k, v, ki


def shared_kv(h, w_kv):
    B, T = h.shape[:2]
    k, v = jnp.split(h @ w_kv, 2, axis=-1)
    return k.reshape(B, T, B_KV_HEADS, HEAD_DIM), v.reshape(B, T, B_KV_HEADS, HEAD_DIM)


def dilated_block(qb, qpos, k_all, v_all, rel_bias):
    lses, outs = [], []
    for g, (window, dil) in enumerate(B_GROUPS):
        dist = jnp.arange(window // dil + 1, dtype=jnp.int32) * dil
        idx = qpos[:, None] - dist[None, :]
        valid = idx >= 0
        idx = jnp.maximum(idx, 0)
        ks, vs = k_all[:, idx], v_all[:, idx]
        logits = jnp.einsum('bqgrd,bqjgd->bqgrj', qb[:, :, g], ks,
                            preferred_element_type=jnp.float32) * HEAD_DIM ** -0.5
        bias = rel_bias[t5_bucket(dist)].T.reshape(B_KV_HEADS, B_GROUP, -1)
        logits = jnp.where(valid[None, :, None, None, :], logits + bias, NEG)
        m = logits.max(-1, keepdims=True)
        e = jnp.exp(logits - m)
        s = e.sum(-1)
        outs.append(jnp.einsum('bqgrj,bqjgd->bqgrd', e, vs.astype(jnp.float32)) / s[..., None])
        lses.append(m[..., 0] + jnp.log(s))
    wgt = jax.nn.softmax(jnp.stack(lses), axis=0)
    return jnp.einsum('nbqgr,nbqgrd->bqgrd', wgt, jnp.stack(outs)).astype(qb.dtype)


def mixer_b(h, w_q, w_o, k_all, v_all, qpos, rel_bias):
    B, T = h.shape[:2]
    q = (h @ w_q).reshape(B, T, N_B_GROUPS, B_KV_HEADS, B_GROUP, HEAD_DIM)
    o = over_query_blocks(lambda qb, p: dilated_block(qb, p, k_all, v_all, rel_bias), (q,), qpos)
    return o.reshape(B, T, -1) @ w_o


def setup_inputs(seed: int = 0) -> dict:
    key = jax.random.key(seed)
    ks = jax.random.split(key, 24)
    f32 = jnp.float32
    nrm = lambda k, shape, s=1.0: s * jax.random.normal(k, shape, f32)
    n_pages = PAST_LEN // PAGE_SIZE
    n_used = DEC_BATCH * n_pages
    n_phys = n_used + max(1, n_used // 4)
    w_buf = min(B_WINDOW_MAX, PAST_LEN)
    a_col = jnp.ones((A_IN,), f32).at[A_SPLITS[1]:A_SPLITS[2]].set(BETA)
    kv_col = jnp.concatenate([jnp.ones((B_KV,), f32), jnp.full((B_KV,), BETA, f32)])
    page_table = jax.random.permutation(ks[5], n_phys)[:n_used].reshape(DEC_BATCH, n_pages).astype(jnp.int32)
    return {
        'x_prompt': nrm(ks[0], (BATCH, SEQ, D_MODEL)),
        'x_sample': nrm(ks[1], (DEC_BATCH, DEC_SEQ, D_MODEL)),
        'cache_k_a': nrm(ks[2], (N_A_LAYERS, n_phys, PAGE_SIZE, A_KV_HEADS, HEAD_DIM)),
        'cache_v_a': nrm(ks[3], (N_A_LAYERS, n_phys, PAGE_SIZE, A_KV_HEADS, HEAD_DIM), BETA),
        'cache_kidx_a': nrm(ks[4], (N_A_LAYERS, n_phys, PAGE_SIZE, IDX_DIM)),
        'cache_k_b': nrm(ks[6], (DEC_BATCH, w_buf, B_KV_HEADS, HEAD_DIM)),
        'cache_v_b': nrm(ks[7], (DEC_BATCH, w_buf, B_KV_HEADS, HEAD_DIM), BETA),
        'state_ffn': nrm(ks[8], (DEPTH, DEC_BATCH, CONV_WIDTH - 1, D_FF)),
        'page_table': page_table,
        'a_w_in': nrm(ks[9], (N_A_LAYERS, D_MODEL, A_IN), D_MODEL ** -0.5) * a_col,
        'a_w_o': nrm(ks[10], (N_A_LAYERS, A_Q, D_MODEL), A_Q ** -0.5 * BETA),
        'a_kn_g': 1.0 + nrm(ks[11], (N_A_LAYERS, IDX_DIM), 0.01),
        'a_kn_b': nrm(ks[12], (N_A_LAYERS, IDX_DIM), 0.01),
        'b_w_kv': nrm(ks[13], (D_MODEL, 2 * B_KV), D_MODEL ** -0.5) * kv_col,
        'b_w_q': nrm(ks[14], (N_B_LAYERS, D_MODEL, B_Q), D_MODEL ** -0.5),
        'b_w_o': nrm(ks[15], (N_B_LAYERS, N_HEADS * HEAD_DIM, D_MODEL), (N_HEADS * HEAD_DIM) ** -0.5 * BETA),
        'ffn_w_up': nrm(ks[16], (DEPTH, D_MODEL, 2 * D_FF), D_MODEL ** -0.5),
        'ffn_conv_w': nrm(ks[17], (DEPTH, CONV_WIDTH, D_FF), CONV_WIDTH ** -0.5),
        'ffn_conv_b': nrm(ks[18], (DEPTH, D_FF), 0.01),
        'ffn_w_down': nrm(ks[19], (DEPTH, D_FF, D_MODEL), D_FF ** -0.5 * BETA),
        'ln_g': 1.0 + nrm(ks[20], (DEPTH, 2, D_MODEL), 0.01),
        'ln_b': nrm(ks[21], (DEPTH, 2, D_MODEL), 0.01),
        'rel_bias': nrm(ks[22], (NUM_BUCKETS, N_HEADS), 0.5),
    }


def reference(x_prompt, x_sample, cache_k_a, cache_v_a, cache_kidx_a, cache_k_b, cache_v_b, state_ffn,
              page_table, a_w_in, a_w_o, a_kn_g, a_kn_b, b_w_kv, b_w_q, b_w_o, ffn_w_up, ffn_conv_w,
              ffn_conv_b, ffn_w_down, ln_g, ln_b, rel_bias):
    B, T = x_prompt.shape[:2]
    TS = x_sample.shape[1]
    hp, hs = x_prompt, x_sample
    zero_state = jnp.zeros((B, CONV_WIDTH - 1, D_FF), x_prompt.dtype)
    ka_p, va_p, kia_p, ka_s, va_s, kia_s, ffn_p, ffn_s = [], [], [], [], [], [], [], []
    for layer in range(DEPTH):
        if layer < N_A_LAYERS:
            a = layer
            mp, k, v, ki = mixer_a_prompt(hp, a_w_in[a], a_w_o[a], a_kn_g[a], a_kn_b[a], rel_bias)
            ka_p.append(k); va_p.append(v); kia_p.append(ki)
            ms, k, v, ki = mixer_a_sample(hs, a_w_in[a], a_w_o[a], a_kn_g[a], a_kn_b[a], cache_k_a[a],
                                          cache_v_a[a], cache_kidx_a[a], page_table, rel_bias)
            ka_s.append(k); va_s.append(v); kia_s.append(ki)
        else:
            if layer == N_A_LAYERS:
                kb_p, vb_p = shared_kv(hp, b_w_kv)
                kb_s, vb_s = shared_kv(hs, b_w_kv)
                kb_all_s = jnp.concatenate([cache_k_b.astype(kb_s.dtype), kb_s], axis=1)
                vb_all_s = jnp.concatenate([cache_v_b.astype(vb_s.dtype), vb_s], axis=1)
            bl = layer - N_A_LAYERS
            mp = mixer_b(hp, b_w_q[bl], b_w_o[bl], kb_p, vb_p, jnp.arange(T), rel_bias)
            ms = mixer_b(hs, b_w_q[bl], b_w_o[bl], kb_all_s, vb_all_s,
                         cache_k_b.shape[1] + jnp.arange(TS), rel_bias)
        hp = layer_norm(ALPHA * hp + mp, ln_g[layer, 0], ln_b[layer, 0])
        hs = layer_norm(ALPHA * hs + ms, ln_g[layer, 0], ln_b[layer, 0])
        fp, sp = conv_ffn(hp, zero_state, ffn_w_up[layer], ffn_conv_w[layer], ffn_conv_b[layer], ffn_w_down[layer])
        fs, ss = conv_ffn(hs, state_ffn[layer], ffn_w_up[layer], ffn_conv_w[layer], ffn_conv_b[layer], ffn_w_down[layer])
        ffn_p.append(sp); ffn_s.append(ss)
        hp = layer_norm(ALPHA * hp + fp, ln_g[layer, 1], ln_b[layer, 1])
        hs = layer_norm(ALPHA * hs + fs, ln_g[layer, 1], ln_b[layer, 1])
    keep = min(B_WINDOW_MAX, T)
    return (hp, hs, jnp.stack(ka_p), jnp.stack(va_p), jnp.stack(kia_p), jnp.stack(ka_s), jnp.stack(va_s),
            jnp.stack(kia_s), kb_p[:, T - keep:], vb_p[:, T - keep:], kb_s, vb_s,
            jnp.stack(ffn_p), jnp.stack(ffn_s))
```

```python
import math
import numpy as np
import concourse.bass as bass
import concourse.mybir as mybir
from concourse.bass_utils import run_bass_kernel_spmd

F32 = mybir.dt.float32
BF16 = mybir.dt.bfloat16
I32 = mybir.dt.int32
ALU = mybir.AluOpType
ACTF = mybir.ActivationFunctionType
AX = mybir.AxisListType

PE, ACT, DVE, POOL, SP = "pe", "act", "dve", "pool", "sp"
COMPUTE = (PE, ACT, DVE, POOL)
NDMA_SEMS = 8


class Op:
    __slots__ = ("eng", "fn", "reads", "writes", "is_dma", "deps", "signal", "sem", "semval", "idx", "extra_waits")

    def __init__(self, eng, fn, reads, writes, is_dma):
        self.eng = eng
        self.fn = fn
        self.reads = reads
        self.writes = writes
        self.is_dma = is_dma
        self.deps = []
        self.signal = False
        self.sem = None
        self.semval = 0
        self.extra_waits = []


class Prog:
    def __init__(self, nc):
        self.nc = nc
        self.ops = []
        self.last_w = {}
        self.readers = {}
        self.barrier_marks = []

    def op(self, eng, fn, reads=(), writes=(), dma=False):
        o = Op(eng, fn, tuple(reads), tuple(writes), dma)
        o.idx = len(self.ops)
        deps = {}
        for r in o.reads:
            w = self.last_w.get(r)
            if w is not None:
                deps[w.idx] = w
        for r in o.writes:
            w = self.last_w.get(r)
            if w is not None:
                deps[w.idx] = w
            for rd in self.readers.get(r, ()):
                deps[rd.idx] = rd
        o.deps = [deps[k] for k in sorted(deps)]
        for r in o.reads:
            self.readers.setdefault(r, []).append(o)
        for r in o.writes:
            self.last_w[r] = o
            self.readers[r] = []
        self.ops.append(o)
        return o

    def barrier(self):
        self.barrier_marks.append(len(self.ops))
        self.last_w = {}
        self.readers = {}

    def emit(self):
        nc = self.nc
        ops = self.ops
        prev = 0
        for mark in self.barrier_marks:
            if mark >= len(ops) or mark == prev:
                continue
            last_by_eng = {}
            dmas = []
            for o in ops[prev:mark]:
                if o.is_dma:
                    dmas.append(o)
                else:
                    last_by_eng[o.eng] = o
            first_after = {}
            for o in ops[mark:]:
                if o.eng not in first_after:
                    first_after[o.eng] = o
                if len(first_after) == 5:
                    break
            for e, fo in first_after.items():
                d = {x.idx: x for x in fo.deps}
                for x in list(last_by_eng.values()) + dmas:
                    d[x.idx] = x
                fo.deps = [d[k] for k in sorted(d)]
            prev = mark
        for o in ops:
            for d in o.deps:
                if d.eng == PE and o.eng == PE and not d.is_dma and not o.is_dma:
                    continue
                d.signal = True
        sems = {e: nc.alloc_semaphore(name=f"s_{e}") for e in COMPUTE}
        dma_sems = {e: [nc.alloc_semaphore(name=f"d_{e}_{i}") for i in range(NDMA_SEMS)] for e in (SP, ACT, POOL)}
        cnt = {e: 0 for e in COMPUTE}
        dma_rr = {e: 0 for e in dma_sems}
        dma_uses = {}
        dma_last = {}
        per_eng = {e: [] for e in (PE, ACT, DVE, POOL, SP)}
        for o in ops:
            if o.is_dma:
                pool = dma_sems[o.eng]
                s = pool[dma_rr[o.eng] % NDMA_SEMS]
                dma_rr[o.eng] += 1
                prevop = dma_last.get(s)
                if prevop is not None:
                    o.extra_waits.append(prevop)
                dma_uses[s] = dma_uses.get(s, 0) + 1
                o.sem = s
                o.semval = 16 * dma_uses[s]
                dma_last[s] = o
            elif o.signal:
                cnt[o.eng] += 1
                o.sem = sems[o.eng]
                o.semval = cnt[o.eng]
            per_eng[o.eng].append(o)
        tails = []
        for e in COMPUTE:
            lo = None
            for o in reversed(per_eng[e]):
                if not o.is_dma:
                    lo = o
                    break
            if lo is not None:
                if lo.sem is None:
                    cnt[e] += 1
                    lo.sem = sems[e]
                    lo.semval = cnt[e]
                    lo.signal = True
                tails.append(lo)
        final_waits = [(o.sem, o.semval) for o in tails]
        for s, o in dma_last.items():
            final_waits.append((s, o.semval))
        self.stats = {e: len(per_eng[e]) for e in per_eng}

        def run_engine(e, eng):
            known = {}
            for o in per_eng[e]:
                for d in list(o.deps) + o.extra_waits:
                    if e == PE and d.eng == PE and not d.is_dma and not o.is_dma:
                        continue
                    if d.sem is None:
                        continue
                    if known.get(d.sem, 0) >= d.semval:
                        continue
                    eng.wait_ge(d.sem, d.semval)
                    known[d.sem] = d.semval
                ins = o.fn(eng)
                if o.is_dma:
                    ins.then_inc(o.sem, 16)
                elif o.signal:
                    ins.then_inc(o.sem, 1)
            if e == POOL:
                for s, v in final_waits:
                    if known.get(s, 0) >= v:
                        continue
                    eng.wait_ge(s, v)

        with nc.Block() as block:
            @block.tensor
            def _(eng):
                run_engine(PE, eng)

            @block.scalar
            def _(eng):
                run_engine(ACT, eng)

            @block.vector
            def _(eng):
                run_engine(DVE, eng)

            @block.gpsimd
            def _(eng):
                run_engine(POOL, eng)

            @block.sync
            def _(eng):
                run_engine(SP, eng)


def _reshape(ap, shape):
    if len(shape) == 2:
        return ap
    if len(shape) == 3:
        return ap.rearrange("p (a b) -> p a b", a=shape[1], b=shape[2])
    if len(shape) == 4:
        return ap.rearrange("p (a b c) -> p a b c", a=shape[1], b=shape[2], c=shape[3])
    raise ValueError(shape)


class Arena:
    def __init__(self, nc, words):
        self.words = words
        self.t = nc.alloc_sbuf_tensor("arena", [128, words], F32)
        self.top = 0
        self.marks = []

    def alloc(self, nwords):
        nwords = (nwords + 7) // 8 * 8
        off = self.top
        self.top += nwords
        assert self.top <= self.words, f"arena overflow {self.top} > {self.words}"
        return off

    def f32(self, shape):
        n = int(np.prod(shape[1:]))
        off = self.alloc(n)
        return _reshape(self.t[0:shape[0], off:off + n], shape)

    def bf16(self, shape):
        n = int(np.prod(shape[1:]))
        assert n % 2 == 0
        off = self.alloc(n // 2)
        return _reshape(self.t[0:shape[0], off:off + n // 2].bitcast(BF16), shape)

    def i32(self, shape):
        n = int(np.prod(shape[1:]))
        off = self.alloc(n)
        return _reshape(self.t[0:shape[0], off:off + n].bitcast(I32), shape)

    def push(self):
        self.marks.append(self.top)

    def pop(self):
        self.top = self.marks.pop()


D = 2048
NT = 1024
TS = 8
TOK = NT + TS
TT = [(0, 512), (512, 512), (1024, 8)]
NCH = 9
DEPTH = 4
NA = 2
HD = 128
NH = 16
IDXD = 64
DFF = 5504
NFC = DFF // 128
A_IN = 4176
PAST = 16384
PAGE = 128
NPAGES = 128
NPHYS = 1280
WBUF = 2048
ALPHA = (2 * DEPTH) ** 0.25
LN_EPS = 1e-5
NEGV = -30000.0
B_GROUPS = ((128, 1), (512, 4), (2048, 16))
ZA = 2176
ZB = 2304
ZS = 16640
import os
KSKIP = set(os.environ.get('KSKIP', '').split(','))
SASTOP = int(os.environ.get('SASTOP', '9'))


def t5_bucket_np(dist):
    dist = np.maximum(dist, 0)
    exact = 16
    far = exact + (np.log(np.maximum(dist, 1).astype(np.float32) / exact) / math.log(2048 / exact) * (32 - exact)).astype(np.int32)
    return np.where(dist < exact, dist, np.minimum(far, 31))


def _jfar():
    b = t5_bucket_np(np.arange(0, PAST + 64))
    dsat = int(np.argmax(b == 31))
    assert (b[dsat:] == 31).all()
    return (PAST - 127 - dsat) // 128 + 1


JFAR = _jfar()


def _onehot_line(Z, valid_fn):
    z = np.arange(Z)
    d = z - 127
    ok = valid_fn(d)
    oh = np.zeros((33, Z), np.float32)
    bk = t5_bucket_np(np.maximum(d, 0))
    oh[bk[ok], z[ok]] = 1.0
    oh[32, z[~ok]] = 1.0
    return oh


def _consts():
    ohA = _onehot_line(ZA, lambda d: d >= 0)
    ohB = np.concatenate([_onehot_line(ZB, lambda d, w=w, r=r: (d >= 0) & (d <= w) & (d % r == 0)) for (w, r) in B_GROUPS], axis=1)
    ohS = _onehot_line(ZS, lambda d: d >= 0)
    q = np.arange(128)
    cmask = np.where(q[None, :] <= q[:, None], 0.0, -1e30).astype(np.float32)
    return {
        "ohA": ohA, "ohB": ohB, "ohS": ohS,
        "Jm": np.ascontiguousarray(np.eye(128, dtype=np.float32)[::-1]),
        "ident": np.eye(128, dtype=np.float32),
        "cmask": cmask,
        "iota": np.arange(128, dtype=np.float32).reshape(128, 1),
    }


class B:
    def __init__(self):
        nc = self.nc = bass.Bass("TRN2", target_bir_lowering=False)
        self.P = Prog(nc)
        self.ar = Arena(nc, 51000)
        self.ps = [nc.alloc_psum_tensor(f"ps{i}", [128, 512], F32) for i in range(8)]
        self.uid = 0
        self.evac_rr = 0
        self.cast_rr = 0

    def din(self, name, shape, dt=F32):
        return self.nc.dram_tensor(name, list(shape), dt, kind="ExternalInput")

    def dout(self, name, shape, dt=F32):
        return self.nc.dram_tensor(name, list(shape), dt, kind="ExternalOutput")

    def dint(self, name, shape, dt=F32):
        return self.nc.dram_tensor(name, list(shape), dt)

    def dma(self, out, in_, reads, writes, eng=SP, nc_ok=False):
        if nc_ok:
            def fn(e):
                with self.nc.allow_non_contiguous_dma(reason="small strided"):
                    return e.dma_start(out=out, in_=in_)
        else:
            def fn(e):
                return e.dma_start(out=out, in_=in_)
        return self.P.op(eng, fn, reads, writes, dma=True)

    def mm(self, out, lhsT, rhs, start, stop, reads, writes):
        return self.P.op(PE, lambda e: e.matmul(out, lhsT=lhsT, rhs=rhs, start=start, stop=stop), reads, writes)

    def act(self, out, in_, func, reads, writes, scale=1.0, bias=0.0):
        return self.P.op(ACT, lambda e: e.activation(out=out, in_=in_, func=func, scale=scale, bias=bias), reads, writes)

    def copy(self, eng, out, in_, reads, writes):
        if eng == ACT:
            return self.P.op(ACT, lambda e: e.activation(out=out, in_=in_, func=ACTF.Copy), reads, writes)
        return self.P.op(eng, lambda e: e.tensor_copy(out=out, in_=in_), reads, writes)

    def tt(self, eng, out, in0, in1, op, reads, writes):
        return self.P.op(eng, lambda e: e.tensor_tensor(out=out, in0=in0, in1=in1, op=op), reads, writes)

    def ts(self, eng, out, in0, s1, s2, op0, op1, reads, writes):
        if op1 is None:
            return self.P.op(eng, lambda e: e.tensor_scalar(out=out, in0=in0, scalar1=s1, scalar2=None, op0=op0), reads, writes)
        return self.P.op(eng, lambda e: e.tensor_scalar(out=out, in0=in0, scalar1=s1, scalar2=s2, op0=op0, op1=op1), reads, writes)

    def stt(self, eng, out, in0, scalar, in1, op0, op1, reads, writes):
        return self.P.op(eng, lambda e: e.scalar_tensor_tensor(out=out, in0=in0, scalar=scalar, in1=in1, op0=op0, op1=op1), reads, writes)

    def evac_eng(self):
        self.evac_rr += 1
        return ACT if self.evac_rr % 2 else DVE

    def wstream_init(self, kch_max, nstage=3, nbf=3):
        ar = self.ar
        self.ws_stage = [ar.f32([128, kch_max, 128]) for _ in range(nstage)]
        self.ws_bf = [ar.bf16([128, kch_max, 128]) for _ in range(nbf)]
        self.ws_q = []
        self.ws_issued = 0
        self.ws_cast = 0

    def wstream_set(self, slabs):
        self.ws_q = list(slabs)
        self.ws_base = self.ws_issued
        assert self.ws_issued == self.ws_cast

    def _ws_issue(self, i):
        t, row0, kch, col0, ncols = self.ws_q[i]
        n = self.ws_base + i
        st = self.ws_stage[n % len(self.ws_stage)]
        src = t.ap()[row0:row0 + kch * 128, col0:col0 + ncols].rearrange("(k p) c -> p k c", p=128)
        self.dma(st[:, 0:kch, 0:ncols], src, reads=[], writes=[("wst", n % len(self.ws_stage))])
        self.ws_issued += 1

    def wget(self, i, depth=2):
        while self.ws_issued - self.ws_base < min(len(self.ws_q), i + 1 + depth):
            self._ws_issue(self.ws_issued - self.ws_base)
        assert self.ws_cast - self.ws_base == i, (self.ws_cast, self.ws_base, i)
        t, row0, kch, col0, ncols = self.ws_q[i]
        n = self.ws_base + i
        si = n % len(self.ws_stage)
        bi = n % len(self.ws_bf)
        st = self.ws_stage[si]
        bf = self.ws_bf[bi]
        self.cast_rr += 1
        eng = POOL
        self.copy(eng, bf[:, 0:kch, 0:ncols], st[:, 0:kch, 0:ncols], reads=[("wst", si)], writes=[("wbf", bi)])
        self.ws_cast += 1
        return bf, ("wbf", bi)

    def build(self):
        nc, P, ar = self.nc, self.P, self.ar
        xT = self.din("xT", [D, TOK])
        stT = self.din("stT", [DEPTH, DFF, 2])
        w_in = self.din("a_w_in", [NA * D, A_IN])
        w_oa = self.din("a_w_o", [NA * D, D])
        kn_g = self.din("a_kn_g", [NA, IDXD])
        kn_b = self.din("a_kn_b", [NA, IDXD])
        w_kv = self.din("b_w_kv", [D, 1024])
        w_q = self.din("b_w_q", [2 * D, 6144])
        w_ob = self.din("b_w_o", [2 * D, D])
        w_up = self.din("ffn_w_up", [DEPTH * D, 2 * DFF])
        cw = self.din("ffn_conv_w", [DEPTH, 3, DFF])
        cb = self.din("ffn_conv_b", [DEPTH, DFF])
        w_dn = self.din("ffn_w_down", [DEPTH * DFF, D])
        ln_g = self.din("ln_g", [DEPTH * 2, D])
        ln_b = self.din("ln_b", [DEPTH * 2, D])
        hflag = self.din("hflag", [128, 1])
        pmask_d = self.din("pmask", [128, 1])
        relb = self.din("rel_bias", [32, 16])
        ohA = self.din("ohA", [33, ZA])
        ohB = self.din("ohB", [33, 3 * ZB])
        ohS = self.din("ohS", [33, ZS])
        Jm_d = self.din("Jm", [128, 128])
        ident_d = self.din("ident", [128, 128])
        cmask_d = self.din("cmask", [128, 128])
        iota_d = self.din("iota", [128, 1])
        ptab = self.din("ptab", [1, NPAGES], I32)
        if "sa" in KSKIP:
            ck_a = cv_a = cki_a = [None, None]
        else:
            ck_a = [self.din(f"cache_k_a{a}", [NPHYS * PAGE, 512]) for a in range(NA)]
            cv_a = [self.din(f"cache_v_a{a}", [NPHYS * PAGE, 512]) for a in range(NA)]
            cki_a = [self.din(f"cache_kidx_a{a}", [NPHYS * PAGE, IDXD]) for a in range(NA)]
        ck_b = self.din("ckb", [WBUF, 512])
        cv_b = self.din("cvb", [WBUF, 512])
        self.cache = (ck_a, cv_a, cki_a, ck_b, cv_b, ptab)
        yT = self.dout("yT", [D, TOK])
        o_k = self.dout("o_k", [NA, TOK, 512])
        o_v = self.dout("o_v", [NA, TOK, 512])
        o_ki = self.dout("o_ki", [NA, TOK, IDXD])
        o_kb = self.dout("o_kb", [TOK, 512])
        o_vb = self.dout("o_vb", [TOK, 512])
        o_ffp = self.dout("o_ffp", [DEPTH, 128, NFC * 2])
        o_ffs = self.dout("o_ffs", [DEPTH, 128, NFC * 2])
        h32_d = self.dint("h32_d", [16, 128, TOK])
        xpre_d = self.dint("xpre_d", [16, 128, TOK])
        snd_g = self.dint("snd_g", [128, NFC * 2])
        rcv_g = self.dint("rcv_g", [256, NFC * 2])
        self.PAIRS = [[0, 1], [2, 3], [4, 5], [6, 7]]
        self.OT_d = self.dint("OT_d", [16, 128, TOK], BF16)
        self.snd_a = self.dint("snd_a", [1024, 1024], BF16)
        self.rcv_a = self.dint("rcv_a", [2048, 1024], BF16)
        self.snd_k = self.dint("snd_k", [128, 1024], BF16)
        self.rcv_k = self.dint("rcv_k", [256, 1024], BF16)
        self.snd_b = self.dint("snd_b", [1024, 1024], BF16)
        self.rcv_b = self.dint("rcv_b", [2048, 1024], BF16)
        self.LA_d = self.dint("LA_d", [16, ZA])
        self.LB_d = self.dint("LB_d", [16, 3 * ZB])
        self.LS_d = self.dint("LS_d", [16, ZS])
        self.scr_d = self.dint("scr_d", [128, 256])
        self.scr2_d = self.dint("scr2_d", [8, 16])
        self.scr3_d = self.dint("scr3_d", [8, 1])
        self.h32_d, self.xpre_d = h32_d, xpre_d

        self.hT_off = ar.top
        hT = ar.bf16([128, 16, TOK])
        self.hT_words = ar.top - self.hT_off
        self.KTbS = ar.bf16([128, 4, TS])
        self.VbS = ar.bf16([TS, 512])
        onesF = ar.f32([128, 128])
        lng = ar.f32([128, 8, 16])
        lnb = ar.f32([128, 8, 16])
        hfl = ar.f32([128, 1])
        P.op(DVE, lambda e: e.memset(onesF, 1.0), [], ["onesF"])
        self.epsc = ar.f32([128, 1])
        P.op(DVE, lambda e: e.memset(self.epsc, LN_EPS), [], ["epsc"])
        self.dma(lng, ln_g.ap().rearrange("l (c p) -> p l c", p=128), [], ["lng"], nc_ok=True)
        self.dma(lnb, ln_b.ap().rearrange("l (c p) -> p l c", p=128), [], ["lnb"], nc_ok=True)
        self.dma(hfl, hflag.ap(), [], ["hfl"])
        self.Jm = ar.f32([128, 128]); self.dma(self.Jm, Jm_d.ap(), [], ["Jm"])
        identf = ar.f32([128, 128]); self.dma(identf, ident_d.ap(), [], ["identf"])
        self.identb = ar.bf16([128, 128]); self.copy(DVE, self.identb, identf, ["identf"], ["identb"])
        self.onesB = ar.bf16([128, 128]); P.op(DVE, lambda e: e.memset(self.onesB, 1.0), [], ["onesB"])
        self.cmask = ar.f32([128, 128]); self.dma(self.cmask, cmask_d.ap(), [], ["cmask"])
        self.pmask = ar.f32([128, 1]); self.dma(self.pmask, pmask_d.ap(), [], ["pmask"])
        self.iota = ar.f32([128, 1]); self.dma(self.iota, iota_d.ap(), [], ["iota"])
        self.n1e29 = ar.f32([128, 1]); P.op(DVE, lambda e: e.memset(self.n1e29, -1e29), [], ["n1e29"])
        self.onesF = onesF
        self.maskB = ar.bf16([128, 128])
        P.op(DVE, lambda e: e.memset(self.maskB, 1.0), [], ["maskB"])
        self.ts(DVE, self.maskB, self.maskB, hfl[:, 0:1], None, ALU.mult, None, ["maskB", "hfl"], ["maskB"])
        ar.push()
        RB = ar.f32([33, 16])
        self.dma(RB[0:32, :], relb.ap(), [], ["RB"])
        P.op(DVE, lambda e: e.memset(RB[32:33, :], NEGV), [], ["RB2"])
        ohs = [ar.f32([33, 512]) for _ in range(2)]
        lns = [ar.f32([16, 512]) for _ in range(2)]
        k = 0
        for (oh, ld, Z) in ((ohA, self.LA_d, ZA), (ohB, self.LB_d, 3 * ZB), (ohS, self.LS_d, ZS)):
            if "il" in KSKIP:
                continue
            for z0 in range(0, Z, 512):
                n = min(512, Z - z0)
                o, l = ohs[k % 2], lns[k % 2]
                self.dma(o[:, 0:n], oh.ap()[:, z0:z0 + n], [], [("ohs", k % 2)])
                self.mm(self.ps[k % 2][0:16, 0:n], RB, o[:, 0:n], True, True, ["RB", "RB2", ("ohs", k % 2)], [f"ps{k % 2}"])
                self.copy(self.evac_eng(), l[:, 0:n], self.ps[k % 2][0:16, 0:n], [f"ps{k % 2}"], [("lns", k % 2)])
                self.dma(ld.ap()[:, z0:z0 + n], l[:, 0:n], [("lns", k % 2)], [("line", id(ld), z0)])
                k += 1
        ar.pop()

        P.barrier()
        ar.push()
        xs = [ar.f32([128, TOK]) for _ in range(2)]
        for cc in range(16):
            x = xs[cc % 2]
            self.dma(x, xT.ap()[cc * 128:(cc + 1) * 128, :], [], [("xs", cc % 2)])
            self.copy(self.evac_eng(), hT[:, cc, :], x, [("xs", cc % 2)], [("hT", cc)])
            self.dma(h32_d.ap()[cc], x, [("xs", cc % 2)], [("h32", cc)])
        ar.pop()
        P.barrier()

        for layer in range(DEPTH):
            if layer < NA:
                self.a_layer(layer, hT, w_in, kn_g, kn_b, o_k, o_v, o_ki)
                self.proj_residual(w_oa, layer * D)
            else:
                if layer == NA:
                    self.shared_kv(hT, w_kv, o_kb, o_vb)
                if "b" not in KSKIP:
                    self.b_layer(layer - NA, hT, w_q)
                self.proj_residual(w_ob, (layer - NA) * D)
            self.layer_norm(xpre_d, h32_d, hT, lng, lnb, layer * 2, onesF)
            self.ffn(layer, hT, w_up, cw, cb, w_dn, stT, h32_d, xpre_d, o_ffp, o_ffs, snd_g, rcv_g, hfl)
            self.layer_norm(xpre_d, h32_d, hT, lng, lnb, layer * 2 + 1, onesF)
        ar.push()
        xs = [ar.f32([128, TOK]) for _ in range(2)]
        for cc in range(16):
            x = xs[cc % 2]
            self.dma(x, h32_d.ap()[cc], [("h32", cc)], [("xs", cc % 2)])
            self.dma(yT.ap()[cc * 128:(cc + 1) * 128, :], x, [("xs", cc % 2)], [("yT", cc)])
        ar.pop()
        P.emit()
        return nc

    def mix_residual_zero(self, h32_d, xpre_d):
        ar, P = self.ar, self.P
        ar.push()
        xs = [ar.f32([128, TOK]) for _ in range(2)]
        for cc in range(16):
            x = xs[cc % 2]
            self.dma(x, h32_d.ap()[cc], [("h32", cc)], [("xs", cc % 2)])
            self.ts(DVE, x, x, ALPHA, None, ALU.mult, None, [("xs", cc % 2)], [("xs", cc % 2)])
            self.dma(xpre_d.ap()[cc], x, [("xs", cc % 2)], [("xpre", cc)])
        ar.pop()
        P.barrier()

    def layer_norm(self, xpre_d, h32_d, hT, lng, lnb, li, onesF):
        ar, P, ps = self.ar, self.P, self.ps
        ar.push()
        X = ar.f32([128, 16, TOK])
        sq = [ar.f32([128, 512]) for _ in range(2)]
        M = ar.f32([128, TOK])
        R = ar.f32([128, TOK])
        for cc in range(16):
            self.dma(X[:, cc, :], xpre_d.ap()[cc], [("xpre", cc)], [("X", cc)])
        k = 0
        for ti, (t0, tn) in enumerate(TT):
            p1, p2 = ps[2 * (ti % 2)], ps[2 * (ti % 2) + 1]
            r1, r2 = f"ps{2 * (ti % 2)}", f"ps{2 * (ti % 2) + 1}"
            for cc in range(16):
                s = sq[k % 2]
                self.act(s[:, 0:tn], X[:, cc, t0:t0 + tn], ACTF.Square, [("X", cc)], [("sq", k % 2)])
                self.mm(p1[:, 0:tn], onesF, X[:, cc, t0:t0 + tn], cc == 0, cc == 15, [("X", cc), "onesF"], [r1])
                self.mm(p2[:, 0:tn], onesF, s[:, 0:tn], cc == 0, cc == 15, [("sq", k % 2), "onesF"], [r2])
                k += 1
            self.ts(DVE, M[:, t0:t0 + tn], p1[:, 0:tn], 1.0 / D, None, ALU.mult, None, [r1], [("M", ti)])
            self.ts(DVE, R[:, t0:t0 + tn], p2[:, 0:tn], 1.0 / D, None, ALU.mult, None, [r2], [("R", ti)])
            self.tt(POOL, sq[0][:, 0:tn], M[:, t0:t0 + tn], M[:, t0:t0 + tn], ALU.mult, [("M", ti)], [("sq", 0)])
            self.tt(DVE, R[:, t0:t0 + tn], R[:, t0:t0 + tn], sq[0][:, 0:tn], ALU.subtract, [("R", ti), ("sq", 0)], [("R", ti)])
            self.act(R[:, t0:t0 + tn], R[:, t0:t0 + tn], ACTF.Sqrt, [("R", ti), "epsc"], [("R", ti)], bias=self.epsc[:, 0:1])
            P.op(DVE, lambda e, t0=t0, tn=tn: e.reciprocal(out=R[:, t0:t0 + tn], in_=R[:, t0:t0 + tn]), [("R", ti)], [("R", ti)])
        for cc in range(16):
            for ti, (t0, tn) in enumerate(TT):
                e1 = DVE if (cc + ti) % 2 else POOL
                self.tt(e1, X[:, cc, t0:t0 + tn], X[:, cc, t0:t0 + tn], M[:, t0:t0 + tn], ALU.subtract, [("X", cc), ("M", ti)], [("X", cc)])
                self.tt(e1, X[:, cc, t0:t0 + tn], X[:, cc, t0:t0 + tn], R[:, t0:t0 + tn], ALU.mult, [("X", cc), ("R", ti)], [("X", cc)])
            self.act(X[:, cc, :], X[:, cc, :], ACTF.Identity, [("X", cc), "lng", "lnb"], [("X", cc)],
                     scale=lng[:, li, cc:cc + 1], bias=lnb[:, li, cc:cc + 1])
            self.copy(DVE, hT[:, cc, :], X[:, cc, :], [("X", cc)], [("hT", cc)])
            self.dma(h32_d.ap()[cc], X[:, cc, :], [("X", cc)], [("h32", cc)])
        ar.pop()
        P.barrier()

    def alias_hT_bf16(self, shape):
        n = int(np.prod(shape[1:]))
        assert n // 2 <= self.hT_words
        ap = self.ar.t[0:shape[0], self.hT_off:self.hT_off + n // 2].bitcast(BF16)
        return _reshape(ap, shape)

    def transpose_to(self, dst, src, nrows, k, sres, dres):
        ps = self.ps
        ncols = src.shape[-1]
        bank = 6 + k % 2
        tp = ps[bank][:, 0:64].bitcast(BF16)
        self.P.op(PE, lambda e: e.transpose(out=tp[0:ncols, 0:nrows], in_=src, identity=self.identb[0:nrows, 0:nrows]),
                  list(sres) + ["identb"], [f"ps{bank}"])
        self.copy(ACT, dst, tp[0:ncols, 0:nrows], [f"ps{bank}"], dres)

    def a_layer(self, layer, hT, w_in, kn_g, kn_b, o_k, o_v, o_ki):
        ar, P = self.ar, self.P
        ar.push()
        QT = ar.bf16([128, 16, TOK])
        KTo = ar.bf16([128, 4, TOK])
        Vb = ar.bf16([128, NCH, 512])
        QITs = ar.bf16([128, 8, TS])
        KITn = ar.bf16([128, TS])
        WIsS = ar.f32([TS, 16])
        ar.push()
        KTall = ar.bf16([128, 4, 2048])
        Vall = ar.bf16([128, 16, 512])
        ar.push()
        QIT = ar.bf16([128, 8, TOK])
        KITo = ar.bf16([128, TOK])
        WIs = ar.f32([128, NCH, 16])
        self.a_inproj(layer, hT, w_in, kn_g, kn_b, o_k, o_v, o_ki, QT, KTo, Vb, QIT, KITo, WIs)
        self.copy(DVE, QITs, QIT[:, :, NT:TOK], [], ["QITs"])
        self.copy(DVE, KITn, KITo[:, NT:TOK], [], ["KITn"])
        self.copy(DVE, WIsS, WIs[0:TS, 8, :], [], ["WIsS"])
        P.barrier()
        if "cx" not in KSKIP:
            self.a_exchange(KTo, Vb, KITo, KTall, Vall)
        selT = self.alias_hT_bf16([128, 8, 16, 128])
        if "ix" not in KSKIP:
            self.a_indexer(QIT, KITo, WIs, selT)
        ar.pop()
        P.barrier()
        if "pa" not in KSKIP:
            self.attn_prompt_a(QT, KTall, Vall, selT)
        ar.pop()
        P.barrier()
        if "sa" not in KSKIP:
            self.attn_sample_a(layer, QT, KTo, Vb, QITs, KITn, WIsS)
        ar.pop()
        P.barrier()

    def a_inproj(self, layer, hT, w_in, kn_g, kn_b, o_k, o_v, o_ki, QT, KTo, Vb, QIT, KITo, WIs):
        ar, P, ps = self.ar, self.P, self.ps
        ar.push()
        self.wstream_init(16)
        tmp4 = [ar.f32([128, 128]) for _ in range(4)]
        KW = ar.f32([128, NCH, 128])
        gb = ar.f32([128, 2, IDXD])
        self.dma(gb[:, 0, :], kn_g.ap()[layer:layer + 1, :].partition_broadcast(128), [], ["gb0"])
        self.dma(gb[:, 1, :], kn_b.ap()[layer:layer + 1, :].partition_broadcast(128), [], ["gb1"])
        slabs = []
        for c in range(33):
            c0 = c * 128
            slabs.append((w_in, layer * D, 16, c0, min(128, A_IN - c0)))
        self.wstream_set(slabs)
        hreads = [("hT", kc) for kc in range(16)]
        k4 = 0
        for c in range(33):
            ncols = slabs[c][4]
            Wb, wres = self.wget(c)
            ws_dst = None
            if c < 16:
                ws_dst, scale = QT[:, c, :], HD ** -0.5
            elif c < 20:
                ws_dst, scale = KTo[:, c - 16, :], 1.0
            elif 24 <= c < 32:
                ws_dst, scale = QIT[:, c - 24, :], IDXD ** -0.5
            if ws_dst is not None:
                for ti, (t0, tn) in enumerate(TT):
                    pb, pr = ps[3 + ti], f"ps{3 + ti}"
                    for kc in range(16):
                        self.mm(pb[:, 0:tn], Wb[:, kc, :], hT[:, kc, t0:t0 + tn], kc == 0, kc == 15, hreads + [wres], [pr])
                    self.act(ws_dst[:, t0:t0 + tn], pb[:, 0:tn], ACTF.Copy, [pr], [("ws", c, ti)], scale=scale)
            if 16 <= c < 24 or c == 32:
                for ch in range(NCH):
                    ntk = 128 if ch < 8 else TS
                    pb, pr = ps[ch % 3], f"ps{ch % 3}"
                    for kc in range(16):
                        self.mm(pb[0:ntk, 0:ncols], hT[:, kc, ch * 128:ch * 128 + ntk], Wb[:, kc, 0:ncols], kc == 0, kc == 15,
                                hreads + [wres], [pr])
                    if c == 32:
                        self.copy(self.evac_eng(), KW[0:ntk, ch, 0:ncols], pb[0:ntk, 0:ncols], [pr], [("KW", ch)])
                    else:
                        t4 = tmp4[k4 % 4]
                        r4 = ("tmp4", k4 % 4)
                        k4 += 1
                        self.copy(ACT, t4[0:ntk, :], pb[0:ntk, 0:128], [pr], [r4])
                        j = (c - 16) % 4
                        od = o_k if c < 20 else o_v
                        self.dma(od.ap()[layer, ch * 128:ch * 128 + ntk, j * 128:(j + 1) * 128], t4[0:ntk, :], [r4], [("okv", c, ch)])
                        if c >= 20:
                            self.copy(DVE, Vb[0:ntk, ch, j * 128:(j + 1) * 128], t4[0:ntk, :], [r4], [("Vb", ch, j)])
        for ch in range(NCH):
            ntk = 128 if ch < 8 else TS
            self.ts(DVE, WIs[0:ntk, ch, :], KW[0:ntk, ch, 64:80], NH ** -0.5, None, ALU.mult, None, [("KW", ch)], [("WIs", ch)])
        st6 = ar.f32([128, NCH, 6])
        mv = ar.f32([128, NCH, 2])
        KIn = ar.f32([128, NCH, IDXD])
        KId = ar.bf16([128, NCH, 128])
        for ch in range(NCH):
            ntk = 128 if ch < 8 else TS
            r = [("KW", ch)]
            P.op(DVE, lambda e, ch=ch, ntk=ntk: e.bn_stats(out=st6[0:ntk, ch, :], in_=KW[0:ntk, ch, 0:IDXD]), r, [("st6", ch)])
            P.op(DVE, lambda e, ch=ch, ntk=ntk: e.bn_aggr(out=mv[0:ntk, ch, :], in_=st6[0:ntk, ch, :]), [("st6", ch)], [("mv", ch)])
            self.act(mv[0:ntk, ch, 1:2], mv[0:ntk, ch, 1:2], ACTF.Sqrt, [("mv", ch), "epsc"], [("mv", ch)], bias=self.epsc[0:ntk, 0:1])
            P.op(DVE, lambda e, ch=ch, ntk=ntk: e.reciprocal(out=mv[0:ntk, ch, 1:2], in_=mv[0:ntk, ch, 1:2]), [("mv", ch)], [("mv", ch)])
            self.ts(DVE, KIn[0:ntk, ch, :], KW[0:ntk, ch, 0:IDXD], mv[0:ntk, ch, 0:1], mv[0:ntk, ch, 1:2], ALU.subtract, ALU.mult,
                    r + [("mv", ch)], [("KIn", ch)])
            self.tt(DVE, KIn[0:ntk, ch, :], KIn[0:ntk, ch, :], gb[0:ntk, 0, :], ALU.mult, [("KIn", ch), "gb0"], [("KIn", ch)])
            self.tt(DVE, KIn[0:ntk, ch, :], KIn[0:ntk, ch, :], gb[0:ntk, 1, :], ALU.add, [("KIn", ch), "gb1"], [("KIn", ch)])
            self.dma(o_ki.ap()[layer, ch * 128:ch * 128 + ntk, :], KIn[0:ntk, ch, :], [("KIn", ch)], [("o_ki", ch)])
            self.copy(DVE, KId[0:ntk, ch, 0:64], KIn[0:ntk, ch, :], [("KIn", ch)], [("KId", ch)])
            self.copy(POOL, KId[0:ntk, ch, 64:128], KIn[0:ntk, ch, :], [("KIn", ch)], [("KId2", ch)])
            self.transpose_to(KITo[:, ch * 128:ch * 128 + ntk], KId[0:ntk, ch, :], ntk, ch, [("KId", ch), ("KId2", ch)], [("KITo", ch)])
        ar.pop()
        P.barrier()

    def a_exchange(self, KTo, Vb, KITo, KTall, Vall):
        P = self.P
        snd, rcv = self.snd_a, self.rcv_a
        parts = []
        for g in range(4):
            self.dma(snd.ap()[g * 128:(g + 1) * 128, :], KTo[:, g, 0:NT], [], [("snd_a", "k", g)], eng=POOL)
            parts.append(("snd_a", "k", g))
        vview = snd.ap()[512:1024, :].rearrange("r (two c) -> (r two) c", two=2)
        for ch in range(8):
            self.dma(vview[ch * 128:(ch + 1) * 128, :], Vb[:, ch, :], [], [("snd_a", "v", ch)], eng=POOL)
            parts.append(("snd_a", "v", ch))
        self.dma(self.snd_k.ap(), KITo[:, 0:NT], [], [("snd_a", "ki")], eng=POOL)
        P.op(POOL, lambda e: e.collective_compute("AllGather", ALU.bypass, replica_groups=self.PAIRS,
                                                  ins=[snd.ap().opt()], outs=[rcv.ap().opt()]), parts, ["rcv_a"])
        P.op(POOL, lambda e: e.collective_compute("AllGather", ALU.bypass, replica_groups=self.PAIRS,
                                                  ins=[self.snd_k.ap().opt()], outs=[self.rcv_k.ap().opt()]), [("snd_a", "ki")], ["rcv_k"])
        rv = rcv.ap()[512:1024, :].rearrange("r (two c) -> (r two) c", two=2)
        for g in range(4):
            self.dma(KTall[:, g, 0:NT], rcv.ap()[g * 128:(g + 1) * 128, :], ["rcv_a"], [("KTall", g, 0)])
            self.dma(KTall[:, g, NT:2 * NT], snd.ap()[g * 128:(g + 1) * 128, :], parts, [("KTall", g, 1)])
        for ch in range(8):
            self.dma(Vall[:, ch, :], rv[ch * 128:(ch + 1) * 128, :], ["rcv_a"], [("Vall", ch)])
            self.dma(Vall[:, 8 + ch, :], vview[ch * 128:(ch + 1) * 128, :], parts, [("Vall", 8 + ch)])

    def a_indexer(self, QIT, KITo, WIs, selT):
        ar, P, ps = self.ar, self.P, self.ps
        rcv, snd = self.rcv_a, self.snd_a
        KITall = ar.bf16([128, 2048])
        for hf in range(2):
            self.dma(KITall[hf * 64:(hf + 1) * 64, 0:NT], self.rcv_k.ap()[hf * 64:(hf + 1) * 64, :], ["rcv_k"], [("KITall", hf, 0)])
            self.dma(KITall[hf * 64:(hf + 1) * 64, NT:2 * NT], self.snd_k.ap()[hf * 64:(hf + 1) * 64, :], [("snd_a", "ki")], [("KITall", hf, 1)])
        kall = [("KITall", a, b) for a in range(2) for b in range(2)]
        I = ar.f32([128, 2048])
        Wk = ar.f32([128, 2048])
        S01 = ar.bf16([128, 2048])
        Rl = [ar.f32([128, 512]) for _ in range(2)]
        m8 = ar.f32([128, 32, 8])
        thr2 = ar.f32([128, 1])
        k = 0
        for i in range(8):
            ncol = NT + 128 * (i + 1)
            nt = (ncol + 511) // 512
            for h in range(16):
                hp = (h % 2) * 64
                w = WIs[:, i, h:h + 1]
                for kt in range(nt):
                    n = min(512, ncol - kt * 512)
                    pb, pr = ps[k % 2], f"ps{k % 2}"
                    rl, rr = Rl[k % 2], ("Rl", k % 2)
                    k += 1
                    self.mm(pb[:, 0:n], QIT[hp:hp + 64, h // 2, i * 128:(i + 1) * 128], KITall[hp:hp + 64, kt * 512:kt * 512 + n],
                            True, True, kall, [pr])
                    self.act(rl[:, 0:n], pb[:, 0:n], ACTF.Relu, [pr], [rr])
                    Ic = I[:, kt * 512:kt * 512 + n]
                    if h == 0:
                        self.ts(DVE, Ic, rl[:, 0:n], w, None, ALU.mult, None, [rr], [("I", kt)])
                    else:
                        self.stt(DVE, Ic, rl[:, 0:n], w, Ic, ALU.mult, ALU.add, [rr, ("I", kt)], [("I", kt)])
            Iall = [("I", kt) for kt in range(4)]
            self.ts(DVE, I[:, 0:NT], I[:, 0:NT], self.pmask[:, 0:1], None, ALU.add, None, Iall + ["pmask"], Iall)
            dc = NT + 128 * i
            self.tt(DVE, I[:, dc:dc + 128], I[:, dc:dc + 128], self.cmask, ALU.add, Iall + ["cmask"], Iall)
            src = I
            sres = Iall
            for r in range(32):
                P.op(DVE, lambda e, r=r, src=src, ncol=ncol: e.max(out=m8[:, r, :], in_=src[:, 0:ncol]), sres, [("m8", r)])
                if r < 31:
                    P.op(DVE, lambda e, r=r, src=src, ncol=ncol: e.match_replace(out=Wk[:, 0:ncol], in_to_replace=m8[:, r, :],
                                                                                    in_values=src[:, 0:ncol], imm_value=-1e30),
                         list(sres) + [("m8", r)], ["Wk"])
                    src = Wk
                    sres = ["Wk"]
            self.ts(DVE, thr2, m8[:, 31, 7:8], -1e29, None, ALU.max, None, [("m8", 31)], ["thr2"])
            self.ts(DVE, S01[:, 0:ncol], I[:, 0:ncol], thr2[:, 0:1], None, ALU.is_ge, None, Iall + ["thr2"], ["S01"])
            for kc in range(9 + i):
                self.transpose_to(selT[:, i, kc, :], S01[:, kc * 128:(kc + 1) * 128], 128, kc, ["S01"], [("selT", i, kc)])

    def build_table(self, dst, dres, line_d, Zrow, row0, nh, z0, W, Hst, tag):
        ps = self.ps
        k = 0
        for x0 in range(0, W, 512):
            n = min(512, W - x0)
            src = bass.AP(line_d, row0 * Zrow + z0 + x0, [[1, 128], [Zrow, nh], [1, n]])
            self.dma(Hst[:, 0:nh, 0:n], src, [], [("Hst", tag)], nc_ok=(n < 128))
            if nh * n <= 512:
                bank = 6 + k % 2
                k += 1
                self.mm(ps[bank][:, 0:nh * n], self.Jm, Hst[:, 0:nh, 0:n], True, True, [("Hst", tag), "Jm"], [f"ps{bank}"])
                self.copy(ACT, dst[:, :, x0:x0 + n], ps[bank][:, 0:nh * n].rearrange("p (a b) -> p a b", a=nh), [f"ps{bank}"], [dres])
            else:
                for r in range(nh):
                    bank = 6 + k % 2
                    k += 1
                    self.mm(ps[bank][:, 0:n], self.Jm, Hst[:, r, 0:n], True, True, [("Hst", tag), "Jm"], [f"ps{bank}"])
                    self.copy(ACT, dst[:, r, x0:x0 + n], ps[bank][:, 0:n], [f"ps{bank}"], [dres])

    def attn_tile(self, k, first, last, lhsT_k, rhs_q, shp, bias, mask, v_lhsT, nkeys, bufs, bank_o, reads):
        ps = self.ps
        Tm, Pe, Pm = bufs
        a, b = shp
        n = a * b
        r3 = lambda ap: ap.rearrange("p (a b) -> p a b", a=a)
        bl = k % 2
        pl, plr = ps[bl], f"ps{bl}"
        po, por = ps[bank_o], f"ps{bank_o}"
        pS, pSr = ps[bank_o + 2], f"ps{bank_o + 2}"
        tm, pe, pm = Tm[k % 2], Pe[k % 2], Pm[k % 2]
        self.mm(pl[0:nkeys, 0:n], lhsT_k, rhs_q, True, True, reads, [plr])
        self.tt(DVE, r3(tm[0:nkeys, 0:n]), r3(pl[0:nkeys, 0:n]), bias, ALU.add, [plr] + reads, [("Tm", k % 2)])
        self.act(pe[0:nkeys, 0:n], tm[0:nkeys, 0:n], ACTF.Exp, [("Tm", k % 2)], [("Pe", k % 2)])
        if mask is not None:
            self.tt(POOL, r3(pm[0:nkeys, 0:n]), r3(pe[0:nkeys, 0:n]), mask, ALU.mult, [("Pe", k % 2)] + reads, [("Pm", k % 2)])
            src, sr = pm, ("Pm", k % 2)
        else:
            src, sr = pe, ("Pe", k % 2)
        self.mm(po[:, 0:n], v_lhsT, src[0:nkeys, 0:n], first, last, [sr] + reads, [por])
        self.mm(pS[:, 0:n], self.onesB[0:nkeys, :], src[0:nkeys, 0:n], first, last, [sr, "onesB"], [pSr])

    def attn_finish(self, gi, shp, bank_o, rcb, Ob, dst_ap):
        ps = self.ps
        a, b = shp
        n = a * b
        po, por = ps[bank_o], f"ps{bank_o}"
        pS, pSr = ps[bank_o + 2], f"ps{bank_o + 2}"
        rc, ob = rcb[gi % 2], Ob[gi % 2]
        self.P.op(DVE, lambda e: e.reciprocal(out=rc[:, 0:n], in_=pS[:, 0:n]), [pSr], [("rc", gi % 2)])
        self.tt(DVE, ob[:, 0:n], po[:, 0:n], rc[:, 0:n], ALU.mult, [por, ("rc", gi % 2)], [("Ob", gi % 2)])
        self.dma(dst_ap, ob[:, 0:n].rearrange("p (a b) -> p a b", a=a), [("Ob", gi % 2)], [("OT_d", self.uid)], nc_ok=True)
        self.uid += 1

    def attn_bufs(self):
        ar = self.ar
        Tm = [ar.f32([128, 512]) for _ in range(2)]
        Pe = [ar.bf16([128, 512]) for _ in range(2)]
        Pm = [ar.bf16([128, 512]) for _ in range(2)]
        rcb = [ar.f32([128, 512]) for _ in range(2)]
        Ob = [ar.bf16([128, 512]) for _ in range(2)]
        return (Tm, Pe, Pm), rcb, Ob

    def attn_prompt_a(self, QT, KTall, Vall, selT):
        ar, P = self.ar, self.P
        ar.push()
        TB = ar.f32([128, 4, 2048])
        Hst = ar.f32([128, 4, 512])
        bufs, rcb, Ob = self.attn_bufs()
        k = 0
        gi = 0
        for g in range(4):
            self.build_table(TB, "TB", self.LA_d, ZA, 4 * g, 4, 0, 2048, Hst, "a")
            for i in range(8):
                nkc = 9 + i
                bank_o = 2 + gi % 2
                for kc in range(nkc):
                    off = 8 + i - kc
                    self.attn_tile(k, kc == 0, kc == nkc - 1,
                                   KTall[:, g, kc * 128:(kc + 1) * 128], QT[:, 4 * g:4 * g + 4, i * 128:(i + 1) * 128], (4, 128),
                                   TB[:, :, off * 128:(off + 1) * 128],
                                   selT[:, i, kc, :].unsqueeze(1).to_broadcast([128, 4, 128]),
                                   Vall[:, kc, g * 128:(g + 1) * 128], 128, bufs, bank_o, ["TB"])
                    k += 1
                dst = self.OT_d.ap()[4 * g:4 * g + 4, :, i * 128:(i + 1) * 128].rearrange("h p t -> p h t")
                self.attn_finish(gi, (4, 128), bank_o, rcb, Ob, dst)
                gi += 1
        ar.pop()

    def attn_sample_a(self, layer, QT, KTo, Vb, QITs, KITn, WIsS):
        ar, P, ps = self.ar, self.P, self.ps
        ck_a, cv_a, cki_a, _, _, ptab = self.cache
        ck_a, cv_a, cki_a = ck_a[layer], cv_a[layer], cki_a[layer]
        ar.push()
        KITp = self.alias_hT_bf16([128, PAST])
        pti = ar.i32([128, NPAGES])
        ptf = ar.f32([128, NPAGES])
        idx = ar.i32([128, NPAGES])
        self.dma(pti, ptab.ap().partition_broadcast(128), [], ["pti"])
        self.copy(DVE, ptf, pti, ["pti"], ["ptf"])
        self.ts(DVE, ptf, ptf, float(PAGE), self.iota[:, 0:1], ALU.mult, ALU.add, ["ptf", "iota"], ["ptf"])
        self.copy(DVE, idx, ptf, ["ptf"], ["idx"])

        def gather(dst, table, j, dres):
            return P.op(POOL, lambda e: e.indirect_dma_start(out=dst, out_offset=None, in_=table.ap(),
                                                             in_offset=bass.IndirectOffsetOnAxis(ap=idx[:, j:j + 1], axis=0)),
                        ["idx"], [dres], dma=True)
        KIg = [ar.f32([128, IDXD]) for _ in range(4)]
        KId = [ar.bf16([128, 128]) for _ in range(4)]
        for j in range(NPAGES):
            b4 = j % 4
            gather(KIg[b4], cki_a, j, ("KIg", b4))
            self.copy(DVE, KId[b4][:, 0:64], KIg[b4], [("KIg", b4)], [("KIdA", b4)])
            self.copy(DVE, KId[b4][:, 64:128], KIg[b4], [("KIg", b4)], [("KIdB", b4)])
            self.transpose_to(KITp[:, j * 128:(j + 1) * 128], KId[b4], 128, j, [("KIdA", b4), ("KIdB", b4)], [("KITp", j)])
        kitp = [("KITp", j) for j in range(NPAGES)]
        if SASTOP <= 1:
            ar.pop()
            return
        Zh = ar.bf16([128, 16, 248])
        P.op(POOL, lambda e: e.memset(Zh.rearrange("p a b -> p (a b)"), 0.0), [], ["Zh"])
        for h in range(16):
            hp = (h % 2) * 64
            self.copy(DVE, Zh[hp:hp + 64, h, 120:128], QITs[hp:hp + 64, h // 2, :], ["Zh", "QITs"], ["Zh"])
        wrep = ar.f32([128, 16])
        self.dma(self.scr2_d.ap(), WIsS, ["WIsS"], ["scr2"])
        for sg in range(16):
            self.dma(wrep[sg * 8:(sg + 1) * 8, :], self.scr2_d.ap(), ["scr2"], [("wrep", sg)])
        wr = [("wrep", sg) for sg in range(16)]
        Is = ar.f32([128, 1024])
        Inew = ar.f32([TS, TS])
        Rl = [ar.f32([128, 512]) for _ in range(2)]
        k = 0
        for h in range(16):
            hp = (h % 2) * 64
            for half in range(2):
                bank = 2 * (h % 2) + half
                pb, pr = ps[bank], f"ps{bank}"
                for sg in range(16):
                    self.mm(pb[:, 0:512], Zh[hp:hp + 64, h, 120 - 8 * sg:248 - 8 * sg],
                            KITp[hp:hp + 64, sg * 1024 + half * 512:sg * 1024 + half * 512 + 512], sg == 0, sg == 15, kitp + ["Zh"], [pr])
                rl, rr = Rl[k % 2], ("Rl", k % 2)
                k += 1
                self.act(rl, pb[:, 0:512], ACTF.Relu, [pr], [rr])
                Ic = Is[:, half * 512:(half + 1) * 512]
                if h == 0:
                    self.ts(DVE, Ic, rl, wrep[:, h:h + 1], None, ALU.mult, None, [rr] + wr, [("Is", half)])
                else:
                    self.stt(DVE, Ic, rl, wrep[:, h:h + 1], Ic, ALU.mult, ALU.add, [rr, ("Is", half)] + wr, [("Is", half)])
            self.mm(ps[4][0:TS, 0:TS], QITs[hp:hp + 64, h // 2, :], KITn[hp:hp + 64, :], True, True, ["QITs", "KITn"], ["ps4"])
            self.act(Rl[k % 2][0:TS, 0:TS], ps[4][0:TS, 0:TS], ACTF.Relu, ["ps4"], [("Rl", k % 2)])
            if h == 0:
                self.ts(DVE, Inew, Rl[k % 2][0:TS, 0:TS], WIsS[:, h:h + 1], None, ALU.mult, None, [("Rl", k % 2), "WIsS"], ["Inew"])
            else:
                self.stt(DVE, Inew, Rl[k % 2][0:TS, 0:TS], WIsS[:, h:h + 1], Inew, ALU.mult, ALU.add, [("Rl", k % 2), "WIsS", "Inew"], ["Inew"])
            k += 1
        self.tt(DVE, Inew, Inew, self.cmask[0:TS, 0:TS], ALU.add, ["Inew", "cmask"], ["Inew"])
        if SASTOP <= 2:
            ar.pop()
            return
        Isr = [("Is", 0), ("Is", 1)]
        cand = ar.f32([128, 256])
        Wk = ar.f32([128, 1024])
        src, sres = Is, Isr
        for r in range(32):
            P.op(DVE, lambda e, r=r, src=src: e.max(out=cand[:, r * 8:(r + 1) * 8], in_=src), sres, [("cand", r)])
            if r < 31:
                P.op(DVE, lambda e, r=r, src=src: e.match_replace(out=Wk, in_to_replace=cand[:, r * 8:(r + 1) * 8], in_values=src,
                                                                   imm_value=-1e30), list(sres) + [("cand", r)], ["Wk"])
                src, sres = Wk, ["Wk"]
        self.dma(self.scr_d.ap(), cand, [("cand", r) for r in range(32)], ["scr"])
        C2 = ar.f32([TS, 4096 + TS])
        W2 = ar.f32([TS, 4096 + TS])
        m8 = ar.f32([TS, 32, 8])
        self.dma(C2[:, 0:4096].rearrange("q (s c) -> q s c", s=16), bass.AP(self.scr_d, 0, [[256, TS], [TS * 256, 16], [1, 256]]), ["scr"], ["C2"])
        self.copy(DVE, C2[:, 4096:4096 + TS], Inew, ["Inew", "C2"], ["C2"])
        src, sres = C2, ["C2"]
        for r in range(32):
            P.op(DVE, lambda e, r=r, src=src: e.max(out=m8[:, r, :], in_=src), sres, [("m8s", r)])
            if r < 31:
                P.op(DVE, lambda e, r=r, src=src: e.match_replace(out=W2, in_to_replace=m8[:, r, :], in_values=src, imm_value=-1e30),
                     list(sres) + [("m8s", r)], ["W2"])
                src, sres = W2, ["W2"]
        thr2 = ar.f32([TS, 1])
        thrr = ar.f32([128, 1])
        self.ts(DVE, thr2, m8[:, 31, 7:8], -1e29, None, ALU.max, None, [("m8s", 31)], ["thr2s"])
        self.dma(self.scr3_d.ap(), thr2, ["thr2s"], ["scr3"])
        for sg in range(16):
            self.dma(thrr[sg * 8:(sg + 1) * 8, :], self.scr3_d.ap(), ["scr3"], [("thrr", sg)])
        S01 = ar.bf16([128, 1024])
        S01n = ar.bf16([TS, TS])
        self.ts(DVE, S01, Is, thrr[:, 0:1], None, ALU.is_ge, None, Isr + [("thrr", sg) for sg in range(16)], ["S01s"])
        self.ts(DVE, S01n, Inew, thr2[:, 0:1], None, ALU.is_ge, None, ["Inew", "thr2s"], ["S01n"])
        selTs = ar.bf16([128, 8, 128])
        selTn = ar.bf16([TS, TS])
        for c in range(8):
            self.transpose_to(selTs[:, c, :], S01[:, c * 128:(c + 1) * 128], 128, c, ["S01s"], [("selTs", c)])
        self.transpose_to(selTn, S01n, TS, 0, ["S01n"], ["selTn"])
        sels = [("selTs", c) for c in range(8)] + ["selTn"]
        if SASTOP <= 3:
            ar.pop()
            return
        NNEAR = NPAGES + 1 - JFAR
        Bnear = ar.f32([128, NNEAR, 16, TS])
        Bfar = ar.f32([128, 16, TS])
        Hs = ar.f32([128, 16, TS])
        self.build_table(Bfar, "Bfar", self.LS_d, ZS, 0, 16, PAST - 128 * (JFAR - 1), TS, Hs, "s")
        for j in range(JFAR, NPAGES + 1):
            self.build_table(Bnear[:, j - JFAR], ("Bnear", j), self.LS_d, ZS, 0, 16, PAST - 128 * j, TS, Hs, "s")
        Kg = [ar.f32([128, 512]) for _ in range(2)]
        Vg = [ar.f32([128, 512]) for _ in range(2)]
        Kb = [ar.bf16([128, 512]) for _ in range(2)]
        Vp = [ar.bf16([128, 512]) for _ in range(2)]
        KTp = [ar.bf16([128, 4, 128]) for _ in range(2)]
        Tm = [ar.f32([128, 128]) for _ in range(2)]
        Pe = [ar.bf16([128, 128]) for _ in range(2)]
        Pm = [ar.bf16([128, 128]) for _ in range(2)]
        r3 = lambda ap: ap.rearrange("p (a b) -> p a b", a=16)
        po, pS = ps[2], ps[3]
        for j in range(NPAGES + 1):
            b2 = j % 2
            if j < NPAGES:
                nk = 128
                gather(Kg[b2], ck_a, j, ("Kg", b2))
                gather(Vg[b2], cv_a, j, ("Vg", b2))
                self.copy(POOL, Kb[b2], Kg[b2], [("Kg", b2)], [("Kb", b2)])
                self.copy(DVE, Vp[b2], Vg[b2], [("Vg", b2)], [("Vp", b2)])
                for g in range(4):
                    self.transpose_to(KTp[b2][:, g, :], Kb[b2][:, g * 128:(g + 1) * 128], 128, g, [("Kb", b2)], [("KTp", b2, g)])
                ktp = lambda g: KTp[b2][:, g, :]
                vp = lambda g: Vp[b2][:, g * 128:(g + 1) * 128]
                kres = [("KTp", b2, g) for g in range(4)]
                vres = [("Vp", b2)]
                bias = Bfar if j < JFAR else Bnear[:, j - JFAR]
                bres = ["Bfar"] if j < JFAR else [("Bnear", j)]
                mask = selTs[:, j % 8, (j // 8) * 8:(j // 8) * 8 + 8].unsqueeze(1).to_broadcast([128, 16, TS])
            else:
                nk = TS
                ktp = lambda g: KTo[:, g, NT:TOK]
                vp = lambda g: Vb[0:TS, 8, g * 128:(g + 1) * 128]
                kres, vres = [], []
                bias = Bnear[0:TS, j - JFAR]
                bres = [("Bnear", j)]
                mask = selTn.unsqueeze(1).to_broadcast([TS, 16, TS])
            if SASTOP <= 5:
                continue
            pl, plr = ps[b2], f"ps{b2}"
            for g in range(4):
                self.mm(pl[0:nk, g * 32:(g + 1) * 32], ktp(g), QT[:, 4 * g:4 * g + 4, NT:TOK], True, True, kres, [plr])
            self.tt(DVE, r3(Tm[b2][0:nk, :]), r3(pl[0:nk, 0:128]), bias, ALU.add, [plr] + bres, [("Tms", b2)])
            self.act(Pe[b2][0:nk, :], Tm[b2][0:nk, :], ACTF.Exp, [("Tms", b2)], [("Pes", b2)])
            self.tt(POOL, r3(Pm[b2][0:nk, :]), r3(Pe[b2][0:nk, :]), mask, ALU.mult, [("Pes", b2)] + sels, [("Pms", b2)])
            for g in range(4):
                self.mm(po[:, g * 32:(g + 1) * 32], vp(g), Pm[b2][0:nk, g * 32:(g + 1) * 32], j == 0, j == NPAGES, [("Pms", b2)] + vres, ["ps2"])
                self.mm(pS[:, g * 32:(g + 1) * 32], self.onesB[0:nk, :], Pm[b2][0:nk, g * 32:(g + 1) * 32], j == 0, j == NPAGES,
                        [("Pms", b2), "onesB"], ["ps3"])
        if SASTOP <= 5:
            ar.pop()
            return
        rc = ar.f32([128, 128])
        Ob = ar.bf16([128, 128])
        P.op(DVE, lambda e: e.reciprocal(out=rc, in_=pS[:, 0:128]), ["ps3"], ["rcs"])
        self.tt(DVE, Ob, po[:, 0:128], rc, ALU.mult, ["ps2", "rcs"], ["Obs"])
        self.dma(self.OT_d.ap()[:, :, NT:TOK].rearrange("h p t -> p h t"), r3(Ob), ["Obs"], [("OT_d", "s")], nc_ok=True)
        ar.pop()

    def shared_kv(self, hT, w_kv, o_kb, o_vb):
        ar, P, ps = self.ar, self.P, self.ps
        ar.push()
        self.wstream_init(16)
        tmp4 = [ar.f32([128, 128]) for _ in range(4)]
        KTbo = ar.bf16([128, 4, TOK])
        Vbb = ar.bf16([128, NCH, 512])
        slabs = [(w_kv, 0, 16, c * 128, 128) for c in range(8)]
        self.wstream_set(slabs)
        hreads = [("hT", kc) for kc in range(16)]
        k4 = 0
        for c in range(8):
            Wb, wres = self.wget(c)
            if c < 4:
                for ti, (t0, tn) in enumerate(TT):
                    pb, pr = ps[3 + ti], f"ps{3 + ti}"
                    for kc in range(16):
                        self.mm(pb[:, 0:tn], Wb[:, kc, :], hT[:, kc, t0:t0 + tn], kc == 0, kc == 15, hreads + [wres], [pr])
                    self.act(KTbo[:, c, t0:t0 + tn], pb[:, 0:tn], ACTF.Copy, [pr], [("KTbo", c, ti)])
            for ch in range(NCH):
                ntk = 128 if ch < 8 else TS
                pb, pr = ps[ch % 3], f"ps{ch % 3}"
                for kc in range(16):
                    self.mm(pb[0:ntk, 0:128], hT[:, kc, ch * 128:ch * 128 + ntk], Wb[:, kc, :], kc == 0, kc == 15, hreads + [wres], [pr])
                t4, r4 = tmp4[k4 % 4], ("tmp4", k4 % 4)
                k4 += 1
                self.copy(ACT, t4[0:ntk, :], pb[0:ntk, 0:128], [pr], [r4])
                j = c % 4
                od = o_kb if c < 4 else o_vb
                self.dma(od.ap()[ch * 128:ch * 128 + ntk, j * 128:(j + 1) * 128], t4[0:ntk, :], [r4], [("okvb", c, ch)])
                if c >= 4:
                    self.copy(DVE, Vbb[0:ntk, ch, j * 128:(j + 1) * 128], t4[0:ntk, :], [r4], [("Vbb", ch, j)])
        P.barrier()
        snd, rcv = self.snd_b, self.rcv_b
        parts = []
        for g in range(4):
            self.dma(snd.ap()[g * 128:(g + 1) * 128, :], KTbo[:, g, 0:NT], [], [("snd_b", "k", g)], eng=POOL)
            parts.append(("snd_b", "k", g))
        vview = snd.ap()[512:1024, :].rearrange("r (two c) -> (r two) c", two=2)
        for ch in range(8):
            self.dma(vview[ch * 128:(ch + 1) * 128, :], Vbb[:, ch, :], [], [("snd_b", "v", ch)], eng=POOL)
            parts.append(("snd_b", "v", ch))
        if "cx" not in KSKIP:
            P.op(POOL, lambda e: e.collective_compute("AllGather", ALU.bypass, replica_groups=self.PAIRS,
                                                      ins=[snd.ap().opt()], outs=[rcv.ap().opt()]), parts, ["rcv_b"])
        self.copy(DVE, self.KTbS, KTbo[:, :, NT:TOK], [], ["KTbS"])
        self.copy(DVE, self.VbS, Vbb[0:TS, 8, :], [], ["VbS"])
        ar.pop()
        P.barrier()

    def b_layer(self, bl, hT, w_q):
        ar, P, ps = self.ar, self.P, self.ps
        _, _, _, ck_b, cv_b, _ = self.cache
        snd, rcv = self.snd_b, self.rcv_b
        ar.push()
        KTall = ar.bf16([128, 4, 2048])
        Vall = ar.bf16([128, 16, 512])
        QTsB = ar.bf16([128, 48, TS])
        rv = rcv.ap()[512:1024, :].rearrange("r (two c) -> (r two) c", two=2)
        vview = snd.ap()[512:1024, :].rearrange("r (two c) -> (r two) c", two=2)
        for g in range(4):
            self.dma(KTall[:, g, 0:NT], rcv.ap()[g * 128:(g + 1) * 128, :], [], [("KTall", g, 0)])
            self.dma(KTall[:, g, NT:2 * NT], snd.ap()[g * 128:(g + 1) * 128, :], [], [("KTall", g, 1)])
        for ch in range(8):
            self.dma(Vall[:, ch, :], rv[ch * 128:(ch + 1) * 128, :], [], [("Vall", ch)])
            self.dma(Vall[:, 8 + ch, :], vview[ch * 128:(ch + 1) * 128, :], [], [("Vall", 8 + ch)])
        kv_res = [("KTall", g, a) for g in range(4) for a in range(2)] + [("Vall", c) for c in range(16)]
        ar.push()
        self.wstream_init(16, nstage=2, nbf=2)
        QTg = ar.bf16([128, 12, TOK])
        WT = (256, 640, 2048)
        TBs = [ar.f32([128, 4, w]) for w in WT]
        Hst = ar.f32([128, 4, 512])
        bufs, rcb, Ob = self.attn_bufs()
        hreads = [("hT", kc) for kc in range(16)]
        k = 0
        gi = 0
        for g in range(4):
            slabs = [(w_q, bl * D, 16, grp * 2048 + (4 * g + r) * 128, 128) for grp in range(3) for r in range(4)]
            self.wstream_set(slabs)
            for si in range(12):
                Wb, wres = self.wget(si, depth=1)
                for ti, (t0, tn) in enumerate(TT):
                    pb, pr = ps[6 + (si * 3 + ti) % 2], f"ps{6 + (si * 3 + ti) % 2}"
                    for kc in range(16):
                        self.mm(pb[:, 0:tn], Wb[:, kc, :], hT[:, kc, t0:t0 + tn], kc == 0, kc == 15, hreads + [wres], [pr])
                    self.act(QTg[:, si, t0:t0 + tn], pb[:, 0:tn], ACTF.Copy, [pr], [("QTg", si)], scale=HD ** -0.5)
                grp, r = si // 4, si % 4
                self.copy(DVE, QTsB[:, grp * 16 + 4 * g + r, :], QTg[:, si, NT:TOK], [("QTg", si)], [("QTsB", grp * 16 + 4 * g + r)])
            qres = [("QTg", si) for si in range(12)]
            for grp in range(3):
                self.build_table(TBs[grp], ("TBb", grp), self.LB_d, 3 * ZB, 4 * g, 4, grp * ZB, WT[grp], Hst, "b")
            for i in range(8):
                tiles = []
                for grp in range(3):
                    noff = WT[grp] // 128
                    for off in range(noff):
                        kc = 8 + i - off
                        if kc >= 0:
                            tiles.append((grp, off, kc))
                bank_o = 2 + gi % 2
                for t, (grp, off, kc) in enumerate(tiles):
                    self.attn_tile(k, t == 0, t == len(tiles) - 1,
                                   KTall[:, g, kc * 128:(kc + 1) * 128], QTg[:, grp * 4:(grp + 1) * 4, i * 128:(i + 1) * 128], (4, 128),
                                   TBs[grp][:, :, off * 128:(off + 1) * 128],
                                   self.maskB.unsqueeze(1).to_broadcast([128, 4, 128]) if kc < 8 else None,
                                   Vall[:, kc, g * 128:(g + 1) * 128], 128, bufs, bank_o, [("TBb", grp)] + qres + kv_res)
                    k += 1
                dst = self.OT_d.ap()[4 * g:4 * g + 4, :, i * 128:(i + 1) * 128].rearrange("h p t -> p h t")
                self.attn_finish(gi, (4, 128), bank_o, rcb, Ob, dst)
                gi += 1
        ar.pop()
        P.barrier()
        ar.push()
        act_list = []
        for kc in range(17):
            for grp, lo in ((0, 15), (1, 12), (2, 0)):
                if kc >= lo:
                    act_list.append((kc, grp))
        BSB = ar.f32([128, len(act_list), 16, TS])
        Hs = ar.f32([128, 16, TS])
        for ti, (kc, grp) in enumerate(act_list):
            self.build_table(BSB[:, ti], ("BSB", ti), self.LB_d, 3 * ZB, 0, 16, grp * ZB + WBUF - 128 * kc, TS, Hs, "sb")
        Kg = [ar.f32([128, 512]) for _ in range(2)]
        Vg = [ar.f32([128, 512]) for _ in range(2)]
        Kb = [ar.bf16([128, 512]) for _ in range(2)]
        Vp = [ar.bf16([128, 512]) for _ in range(2)]
        KTp = [ar.bf16([128, 4, 128]) for _ in range(2)]
        Tm = [ar.f32([128, 128]) for _ in range(2)]
        Pe = [ar.bf16([128, 128]) for _ in range(2)]
        r3 = lambda ap: ap.rearrange("p (a b) -> p a b", a=16)
        po, pS = ps[2], ps[3]
        n_t = len(act_list)
        tix = 0
        for kc in range(17):
            b2 = kc % 2
            if kc < 16:
                nk = 128
                self.dma(Kg[b2], ck_b.ap()[kc * 128:(kc + 1) * 128, :], [], [("Kg", b2)])
                self.dma(Vg[b2], cv_b.ap()[kc * 128:(kc + 1) * 128, :], [], [("Vg", b2)])
                self.copy(POOL, Kb[b2], Kg[b2], [("Kg", b2)], [("Kb", b2)])
                self.copy(DVE, Vp[b2], Vg[b2], [("Vg", b2)], [("Vp", b2)])
                for g in range(4):
                    self.transpose_to(KTp[b2][:, g, :], Kb[b2][:, g * 128:(g + 1) * 128], 128, g, [("Kb", b2)], [("KTp", b2, g)])
                ktp = lambda g: KTp[b2][:, g, :]
                vp = lambda g: Vp[b2][:, g * 128:(g + 1) * 128]
                kres = [("KTp", b2, g) for g in range(4)]
                vres = [("Vp", b2)]
            else:
                nk = TS
                ktp = lambda g: self.KTbS[:, g, :]
                vp = lambda g: self.VbS[:, g * 128:(g + 1) * 128]
                kres, vres = ["KTbS"], ["VbS"]
            for grp in range(3):
                if (kc, grp) not in act_list:
                    continue
                ti = act_list.index((kc, grp))
                tb = tix % 2
                pl, plr = ps[tb], f"ps{tb}"
                for g in range(4):
                    self.mm(pl[0:nk, g * 32:(g + 1) * 32], ktp(g), QTsB[:, grp * 16 + 4 * g:grp * 16 + 4 * g + 4, :], True, True, kres, [plr])
                self.tt(DVE, r3(Tm[tb][0:nk, :]), r3(pl[0:nk, 0:128]), BSB[0:nk, ti], ALU.add, [plr, ("BSB", ti)], [("Tms", tb)])
                self.act(Pe[tb][0:nk, :], Tm[tb][0:nk, :], ACTF.Exp, [("Tms", tb)], [("Pes", tb)])
                for g in range(4):
                    self.mm(po[:, g * 32:(g + 1) * 32], vp(g), Pe[tb][0:nk, g * 32:(g + 1) * 32], tix == 0, tix == n_t - 1, [("Pes", tb)] + vres, ["ps2"])
                    self.mm(pS[:, g * 32:(g + 1) * 32], self.onesB[0:nk, :], Pe[tb][0:nk, g * 32:(g + 1) * 32], tix == 0, tix == n_t - 1,
                            [("Pes", tb), "onesB"], ["ps3"])
                tix += 1
        rc = ar.f32([128, 128])
        Ob = ar.bf16([128, 128])
        P.op(DVE, lambda e: e.reciprocal(out=rc, in_=pS[:, 0:128]), ["ps3"], ["rcs"])
        self.tt(DVE, Ob, po[:, 0:128], rc, ALU.mult, ["ps2", "rcs"], ["Obs"])
        self.dma(self.OT_d.ap()[:, :, NT:TOK].rearrange("h p t -> p h t"), r3(Ob), ["Obs"], [("OT_d", "s")], nc_ok=True)
        ar.pop()
        ar.pop()
        P.barrier()

    def proj_residual(self, w, row0):
        ar, P, ps = self.ar, self.P, self.ps
        h32_d, xpre_d = self.h32_d, self.xpre_d
        ar.push()
        OTs = ar.bf16([128, 16, TOK])
        for h in range(16):
            self.dma(OTs[:, h, :], self.OT_d.ap()[h], [], [("OTs", h)])
        oreads = [("OTs", h) for h in range(16)]
        self.wstream_init(16, nstage=2, nbf=2)
        hs = [ar.f32([128, TOK]) for _ in range(2)]
        slabs = [(w, row0, 16, cc * 128, 128) for cc in range(16)]
        self.wstream_set(slabs)
        for cc in range(16):
            Wd, rd = self.wget(cc, depth=1)
            hx = hs[cc % 2]
            self.dma(hx, h32_d.ap()[cc], [("h32", cc)], [("hs", cc % 2)])
            for ti, (t0, tn) in enumerate(TT):
                pb, pr = ps[ti + 3 * (cc % 2)], f"ps{ti + 3 * (cc % 2)}"
                for h in range(16):
                    self.mm(pb[:, 0:tn], Wd[:, h, :], OTs[:, h, t0:t0 + tn], h == 0, h == 15, oreads + [rd], [pr])
                self.stt(DVE, hx[:, t0:t0 + tn], hx[:, t0:t0 + tn], ALPHA, pb[:, 0:tn], ALU.mult, ALU.add, [("hs", cc % 2), pr], [("hs", cc % 2)])
            self.dma(xpre_d.ap()[cc], hx, [("hs", cc % 2)], [("xpre", cc)])
        ar.pop()
        P.barrier()

    def ffn(self, layer, hT, w_up, cw, cb, w_dn, stT, h32_d, xpre_d, o_ffp, o_ffs, snd_g, rcv_g, hfl):
        ar, P, ps, nc = self.ar, self.P, self.ps, self.nc
        ar.push()
        actT = ar.bf16([128, NFC, TOK])
        cwt = ar.f32([128, 3, NFC])
        cbt = ar.f32([128, NFC])
        stt_ = ar.f32([128, NFC, 2])
        GL = ar.f32([128, NFC, 2])
        GS = ar.f32([128, NFC, 2])
        G01 = ar.f32([128, NFC, 2])
        U01 = ar.f32([128, NFC, 2])
        HAL = ar.f32([128, NFC, 2])
        self.dma(cwt, cw.ap()[layer].rearrange("j (c p) -> p j c", p=128), [], ["cwt"], nc_ok=True)
        self.dma(cbt, cb.ap()[layer].rearrange("(c p) -> p c", p=128), [], ["cbt"], nc_ok=True)
        self.dma(stt_, stT.ap()[layer].rearrange("(c p) j -> p c j", p=128), [], ["stt"], nc_ok=True)
        ar.push()
        self.wstream_init(16)
        GXs = [ar.f32([128, 2 + NT + 2 + TS]) for _ in range(2)]
        CVs = [ar.f32([128, TOK]) for _ in range(2)]
        SGs = [ar.f32([128, TOK]) for _ in range(2)]
        for i in range(2):
            P.op(DVE, lambda e, i=i: e.memset(GXs[i][:, 0:2], 0.0), [], [("GX", i)])
        slabs = []
        for fc in range(NFC):
            slabs.append((w_up, layer * D, 16, fc * 128, 128))
            slabs.append((w_up, layer * D, 16, DFF + fc * 128, 128))
        self.wstream_set(slabs)
        hreads = [("hT", kc) for kc in range(16)]
        for fc in range(NFC):
            Wg, rg = self.wget(2 * fc)
            Wu, ru = self.wget(2 * fc + 1)
            GX = GXs[fc % 2]
            CV = CVs[fc % 2]
            SG = SGs[fc % 2]
            gxr, cvr, sgr = ("GX", fc % 2), ("CV", fc % 2), ("SG", fc % 2)
            self.copy(POOL, GX[:, 2 + NT:2 + NT + 2], stt_[:, fc, :], ["stt"], [gxr])
            for ti, (t0, tn) in enumerate(TT):
                pg, pu = ps[ti], ps[3 + ti]
                rpg, rpu = f"ps{ti}", f"ps{3 + ti}"
                for kc in range(16):
                    self.mm(pg[:, 0:tn], Wg[:, kc, :], hT[:, kc, t0:t0 + tn], kc == 0, kc == 15, hreads + [rg], [rpg])
                for kc in range(16):
                    self.mm(pu[:, 0:tn], Wu[:, kc, :], hT[:, kc, t0:t0 + tn], kc == 0, kc == 15, hreads + [ru], [rpu])
                g0 = 2 + t0 if ti < 2 else 2 + NT + 2
                self.copy(ACT, GX[:, g0:g0 + tn], pg[:, 0:tn], [rpg], [gxr])
            for (c0, n, g0) in ((0, NT, 0), (NT, TS, 2 + NT)):
                self.ts(DVE, CV[:, c0:c0 + n], GX[:, g0:g0 + n], cwt[:, 0, fc:fc + 1], cbt[:, fc:fc + 1], ALU.mult, ALU.add,
                        [gxr, "cwt", "cbt"], [cvr])
                self.stt(DVE, CV[:, c0:c0 + n], GX[:, g0 + 1:g0 + 1 + n], cwt[:, 1, fc:fc + 1], CV[:, c0:c0 + n], ALU.mult, ALU.add,
                         [gxr, "cwt", cvr], [cvr])
                self.stt(DVE, CV[:, c0:c0 + n], GX[:, g0 + 2:g0 + 2 + n], cwt[:, 2, fc:fc + 1], CV[:, c0:c0 + n], ALU.mult, ALU.add,
                         [gxr, "cwt", cvr], [cvr])
            self.act(SG, CV, ACTF.Silu, [cvr], [sgr])
            for ti, (t0, tn) in enumerate(TT):
                self.tt(DVE, actT[:, fc, t0:t0 + tn], SG[:, t0:t0 + tn], ps[3 + ti][:, 0:tn], ALU.mult, [sgr, f"ps{3 + ti}"], [("actT", fc)])
            self.copy(POOL, GL[:, fc, :], GX[:, 2 + NT - 2:2 + NT], [gxr], [("GL", fc)])
            self.copy(POOL, GS[:, fc, :], GX[:, 2 + NT + 2 + TS - 2:2 + NT + 2 + TS], [gxr], [("GS", fc)])
            self.copy(POOL, G01[:, fc, :], GX[:, 2:4], [gxr], [("G01", fc)])
            self.copy(ACT, U01[:, fc, :], ps[3][:, 0:2], ["ps3"], [("U01", fc)])
        ar.pop()
        allfc = lambda n: [(n, fc) for fc in range(NFC)]
        self.dma(o_ffp.ap()[layer], GL.rearrange("p a b -> p (a b)"), allfc("GL"), [("o_ffp", layer)])
        self.dma(o_ffs.ap()[layer], GS.rearrange("p a b -> p (a b)"), allfc("GS"), [("o_ffs", layer)])
        self.dma(snd_g.ap(), GL.rearrange("p a b -> p (a b)"), allfc("GL"), ["snd_g"], eng=POOL)
        P.op(POOL, lambda e: e.collective_compute("AllGather", ALU.bypass, replica_groups=self.PAIRS,
                                                  ins=[snd_g.ap().opt()], outs=[rcv_g.ap().opt()]), ["snd_g"], ["rcv_g"])
        self.dma(HAL.rearrange("p a b -> p (a b)"), rcv_g.ap()[0:128, :], ["rcv_g"], ["HAL"], eng=POOL)
        FX = ar.f32([128, NFC, 4])
        CF = ar.f32([128, NFC, 2])
        self.ts(DVE, FX[:, :, 0:2], HAL, hfl[:, 0:1], None, ALU.mult, None, ["HAL", "hfl"], ["FX"])
        self.copy(DVE, FX[:, :, 2:4], G01, allfc("G01") + ["FX"], ["FX"])
        for t in range(2):
            self.tt(DVE, CF[:, :, t], FX[:, :, t], cwt[:, 0, :], ALU.mult, ["FX", "cwt"], ["CF"])
            self.tt(DVE, CF[:, :, t], CF[:, :, t], cbt, ALU.add, ["CF", "cbt"], ["CF"])
            for j in (1, 2):
                self.tt(DVE, FX[:, :, t] if False else HAL[:, :, 0], FX[:, :, t + j], cwt[:, j, :], ALU.mult, ["FX", "cwt", "HAL", "CF"], ["HAL"])
                self.tt(DVE, CF[:, :, t], CF[:, :, t], HAL[:, :, 0], ALU.add, ["CF", "HAL"], ["CF"])
        self.act(CF, CF, ACTF.Silu, ["CF"], ["CF"])
        self.tt(DVE, CF, CF, U01, ALU.mult, ["CF"] + allfc("U01"), ["CF"])
        self.copy(DVE, actT[:, :, 0:2], CF, ["CF"] + allfc("actT"), allfc("actT"))
        ar.push()
        self.wstream_init(22, nstage=3, nbf=3)
        hs = [ar.f32([128, TOK]) for _ in range(2)]
        FH = [(0, 22), (22, 21)]
        slabs = [(w_dn, layer * DFF + f0 * 128, fn, cc * 128, 128) for cc in range(16) for (f0, fn) in FH]
        self.wstream_set(slabs)
        areads = allfc("actT")
        for cc in range(16):
            Wds = [self.wget(2 * cc, depth=2), self.wget(2 * cc + 1, depth=2)]
            hx = hs[cc % 2]
            self.dma(hx, h32_d.ap()[cc], [("h32", cc)], [("hs", cc % 2)])
            for ti, (t0, tn) in enumerate(TT):
                pb, pr = ps[ti + 3 * (cc % 2)], f"ps{ti + 3 * (cc % 2)}"
                for hi, (f0, fn) in enumerate(FH):
                    Wd, rd = Wds[hi]
                    for fl in range(fn):
                        fc = f0 + fl
                        self.mm(pb[:, 0:tn], Wd[:, fl, :], actT[:, fc, t0:t0 + tn], fc == 0, fc == NFC - 1, areads + [rd], [pr])
                self.stt(DVE, hx[:, t0:t0 + tn], hx[:, t0:t0 + tn], ALPHA, pb[:, 0:tn], ALU.mult, ALU.add, [("hs", cc % 2), pr], [("hs", cc % 2)])
            self.dma(xpre_d.ap()[cc], hx, [("hs", cc % 2)], [("xpre", cc)])
        ar.pop()
        ar.pop()
        P.barrier()


_NC = None


def _get_nc():
    global _NC
    if _NC is None:
        _NC = B().build()
    return _NC


def kernel(**inp):
    nc = _get_nc()
    f = lambda a: np.ascontiguousarray(np.asarray(a, dtype=np.float32))
    xp, xs = np.asarray(inp["x_prompt"]), np.asarray(inp["x_sample"])
    shared = {
        "a_w_in": f(inp["a_w_in"]).reshape(NA * D, A_IN),
        "a_w_o": f(inp["a_w_o"]).reshape(NA * D, D),
        "a_kn_g": f(inp["a_kn_g"]), "a_kn_b": f(inp["a_kn_b"]),
        "b_w_kv": f(inp["b_w_kv"]),
        "b_w_q": f(inp["b_w_q"]).reshape(2 * D, 6144),
        "b_w_o": f(inp["b_w_o"]).reshape(2 * D, D),
        "ffn_w_up": f(inp["ffn_w_up"]).reshape(DEPTH * D, 2 * DFF),
        "ffn_conv_w": f(inp["ffn_conv_w"]), "ffn_conv_b": f(inp["ffn_conv_b"]),
        "ffn_w_down": f(inp["ffn_w_down"]).reshape(DEPTH * DFF, D),
        "ln_g": f(inp["ln_g"]).reshape(DEPTH * 2, D), "ln_b": f(inp["ln_b"]).reshape(DEPTH * 2, D),
        "rel_bias": f(inp["rel_bias"]),
    }
    for a in range(NA):
        shared[f"cache_k_a{a}"] = f(inp["cache_k_a"][a]).reshape(NPHYS * PAGE, 512)
        shared[f"cache_v_a{a}"] = f(inp["cache_v_a"][a]).reshape(NPHYS * PAGE, 512)
        shared[f"cache_kidx_a{a}"] = f(inp["cache_kidx_a"][a]).reshape(NPHYS * PAGE, IDXD)
    shared.update(_consts())
    ckb = f(inp["cache_k_b"]).reshape(8, WBUF, 512)
    cvb = f(inp["cache_v_b"]).reshape(8, WBUF, 512)
    pt = np.ascontiguousarray(np.asarray(inp["page_table"], dtype=np.int32))
    in_maps = []
    for c in range(8):
        b, half = c // 2, c % 2
        xT = np.concatenate([xp[b, half * NT:(half + 1) * NT].T, xs[c].T], axis=1)
        m = dict(shared)
        m["xT"] = f(xT)
        m["stT"] = f(np.transpose(np.asarray(inp["state_ffn"])[:, c], (0, 2, 1)))
        m["hflag"] = np.full((128, 1), float(half), np.float32)
        m["pmask"] = np.full((128, 1), 0.0 if half else -1e30, np.float32)
        m["ptab"] = pt[c:c + 1]
        m["ckb"] = ckb[c]
        m["cvb"] = cvb[c]
        in_maps.append(m)
    used = set(t for t in _input_names(nc))
    if "sa" in KSKIP:
        used = set(u for u in used if not u.startswith("cache_"))
    in_maps = [{k: v for k, v in m.items() if k in used} for m in in_maps]
    res = run_bass_kernel_spmd(nc, in_maps, core_ids=list(range(8))).results
    y_p = np.zeros((4, 2048, D), np.float32)
    y_s = np.zeros((8, 8, D), np.float32)
    k_p = np.zeros((NA, 4, 2048, 4, 128), np.float32); v_p = np.zeros_like(k_p)
    ki_p = np.zeros((NA, 4, 2048, IDXD), np.float32)
    k_s = np.zeros((NA, 8, 8, 4, 128), np.float32); v_s = np.zeros_like(k_s)
    ki_s = np.zeros((NA, 8, 8, IDXD), np.float32)
    kb_p = np.zeros((4, 2048, 4, 128), np.float32); vb_p = np.zeros_like(kb_p)
    kb_s = np.zeros((8, 8, 4, 128), np.float32); vb_s = np.zeros_like(kb_s)
    ff_p = np.zeros((DEPTH, 4, 2, DFF), np.float32)
    ff_s = np.zeros((DEPTH, 8, 2, DFF), np.float32)
    for c in range(8):
        r = res[c]
        b, half = c // 2, c % 2
        sl = slice(half * NT, (half + 1) * NT)
        y_p[b, sl] = r["yT"][:, :NT].T
        y_s[c] = r["yT"][:, NT:].T
        k_p[:, b, sl] = r["o_k"][:, :NT].reshape(NA, NT, 4, 128)
        v_p[:, b, sl] = r["o_v"][:, :NT].reshape(NA, NT, 4, 128)
        ki_p[:, b, sl] = r["o_ki"][:, :NT]
        k_s[:, c] = r["o_k"][:, NT:].reshape(NA, TS, 4, 128)
        v_s[:, c] = r["o_v"][:, NT:].reshape(NA, TS, 4, 128)
        ki_s[:, c] = r["o_ki"][:, NT:]
        kb_p[b, sl] = r["o_kb"][:NT].reshape(NT, 4, 128)
        vb_p[b, sl] = r["o_vb"][:NT].reshape(NT, 4, 128)
        kb_s[c] = r["o_kb"][NT:].reshape(TS, 4, 128)
        vb_s[c] = r["o_vb"][NT:].reshape(TS, 4, 128)
        fs = r["o_ffs"].reshape(DEPTH, 128, NFC, 2).transpose(0, 3, 2, 1).reshape(DEPTH, 2, DFF)
        ff_s[:, c] = fs
        if half == 1:
            fp = r["o_ffp"].reshape(DEPTH, 128, NFC, 2).transpose(0, 3, 2, 1).reshape(DEPTH, 2, DFF)
            ff_p[:, b] = fp
    return (y_p, y_s, k_p, v_p, ki_p, k_s, v_s, ki_s, kb_p, vb_p, kb_s, vb_s, ff_p, ff_s)


def _input_names(nc):
    return _INPUTS


_INPUTS = ["xT", "stT", "a_w_in", "a_w_o", "a_kn_g", "a_kn_b", "b_w_kv", "b_w_q", "b_w_o", "ffn_w_up", "ffn_conv_w",
           "ffn_conv_b", "ffn_w_down", "ln_g", "ln_b", "hflag", "pmask", "rel_bias", "ohA", "ohB", "ohS", "Jm", "ident", "cmask",
           "iota", "ptab", "cache_k_a0", "cache_v_a0", "cache_kidx_a0", "cache_k_a1", "cache_v_a1", "cache_kidx_a1", "ckb", "cvb"]
```

```python
import math
import numpy as np
import concourse.bass as bass
import concourse.mybir as mybir
from concourse.bass_utils import run_bass_kernel_spmd

F32 = mybir.dt.float32
BF16 = mybir.dt.bfloat16
I32 = mybir.dt.int32
ALU = mybir.AluOpType
ACTF = mybir.ActivationFunctionType
AX = mybir.AxisListType

PE, ACT, DVE, POOL, SP = "pe", "act", "dve", "pool", "sp"
COMPUTE = (PE, ACT, DVE, POOL)
NDMA_SEMS = 8


class Op:
    __slots__ = ("eng", "fn", "reads", "writes", "is_dma", "deps", "signal", "sem", "semval", "idx", "extra_waits")

    def __init__(self, eng, fn, reads, writes, is_dma):
        self.eng = eng
        self.fn = fn
        self.reads = reads
        self.writes = writes
        self.is_dma = is_dma
        self.deps = []
        self.signal = False
        self.sem = None
        self.semval = 0
        self.extra_waits = []


class Prog:
    def __init__(self, nc):
        self.nc = nc
        self.ops = []
        self.last_w = {}
        self.readers = {}
        self.barrier_marks = []

    def op(self, eng, fn, reads=(), writes=(), dma=False):
        o = Op(eng, fn, tuple(reads), tuple(writes), dma)
        o.idx = len(self.ops)
        deps = {}
        for r in o.reads:
            w = self.last_w.get(r)
            if w is not None:
                deps[w.idx] = w
        for r in o.writes:
            w = self.last_w.get(r)
            if w is not None:
                deps[w.idx] = w
            for rd in self.readers.get(r, ()):
                deps[rd.idx] = rd
        o.deps = [deps[k] for k in sorted(deps)]
        for r in o.reads:
            self.readers.setdefault(r, []).append(o)
        for r in o.writes:
            self.last_w[r] = o
            self.readers[r] = []
        self.ops.append(o)
        return o

    def barrier(self):
        self.barrier_marks.append(len(self.ops))
        self.last_w = {}
        self.readers = {}

    def emit(self):
        nc = self.nc
        ops = self.ops
        prev = 0
        for mark in self.barrier_marks:
            if mark >= len(ops) or mark == prev:
                continue
            last_by_eng = {}
            dmas = []
            for o in ops[prev:mark]:
                if o.is_dma:
                    dmas.append(o)
                else:
                    last_by_eng[o.eng] = o
            first_after = {}
            for o in ops[mark:]:
                if o.eng not in first_after:
                    first_after[o.eng] = o
                if len(first_after) == 5:
                    break
            for e, fo in first_after.items():
                d = {x.idx: x for x in fo.deps}
                for x in list(last_by_eng.values()) + dmas:
                    d[x.idx] = x
                fo.deps = [d[k] for k in sorted(d)]
            prev = mark
        for o in ops:
            for d in o.deps:
                if d.eng == PE and o.eng == PE and not d.is_dma and not o.is_dma:
                    continue
                d.signal = True
        sems = {e: nc.alloc_semaphore(name=f"s_{e}") for e in COMPUTE}
        dma_sems = {e: [nc.alloc_semaphore(name=f"d_{e}_{i}") for i in range(NDMA_SEMS)] for e in (SP, ACT, POOL)}
        cnt = {e: 0 for e in COMPUTE}
        dma_rr = {e: 0 for e in dma_sems}
        dma_uses = {}
        dma_last = {}
        per_eng = {e: [] for e in (PE, ACT, DVE, POOL, SP)}
        for o in ops:
            if o.is_dma:
                pool = dma_sems[o.eng]
                s = pool[dma_rr[o.eng] % NDMA_SEMS]
                dma_rr[o.eng] += 1
                prevop = dma_last.get(s)
                if prevop is not None:
                    o.extra_waits.append(prevop)
                dma_uses[s] = dma_uses.get(s, 0) + 1
                o.sem = s
                o.semval = 16 * dma_uses[s]
                dma_last[s] = o
            elif o.signal:
                cnt[o.eng] += 1
                o.sem = sems[o.eng]
                o.semval = cnt[o.eng]
            per_eng[o.eng].append(o)
        tails = []
        for e in COMPUTE:
            lo = None
            for o in reversed(per_eng[e]):
                if not o.is_dma:
                    lo = o
                    break
            if lo is not None:
                if lo.sem is None:
                    cnt[e] += 1
                    lo.sem = sems[e]
                    lo.semval = cnt[e]
                    lo.signal = True
                tails.append(lo)
        final_waits = [(o.sem, o.semval) for o in tails]
        for s, o in dma_last.items():
            final_waits.append((s, o.semval))
        self.stats = {e: len(per_eng[e]) for e in per_eng}

        def run_engine(e, eng):
            known = {}
            for o in per_eng[e]:
                for d in list(o.deps) + o.extra_waits:
                    if e == PE and d.eng == PE and not d.is_dma and not o.is_dma:
                        continue
                    if d.sem is None:
                        continue
                    if known.get(d.sem, 0) >= d.semval:
                        continue
                    eng.wait_ge(d.sem, d.semval)
                    known[d.sem] = d.semval
                ins = o.fn(eng)
                if o.is_dma:
                    ins.then_inc(o.sem, 16)
                elif o.signal:
                    ins.then_inc(o.sem, 1)
            if e == POOL:
                for s, v in final_waits:
                    if known.get(s, 0) >= v:
                        continue
                    eng.wait_ge(s, v)

        with nc.Block() as block:
            @block.tensor
            def _(eng):
                run_engine(PE, eng)

            @block.scalar
            def _(eng):
                run_engine(ACT, eng)

            @block.vector
            def _(eng):
                run_engine(DVE, eng)

            @block.gpsimd
            def _(eng):
                run_engine(POOL, eng)

            @block.sync
            def _(eng):
                run_engine(SP, eng)


def _reshape(ap, shape):
    if len(shape) == 2:
        return ap
    if len(shape) == 3:
        return ap.rearrange("p (a b) -> p a b", a=shape[1], b=shape[2])
    if len(shape) == 4:
        return ap.rearrange("p (a b c) -> p a b c", a=shape[1], b=shape[2], c=shape[3])
    raise ValueError(shape)


class Arena:
    def __init__(self, nc, words):
        self.words = words
        self.t = nc.alloc_sbuf_tensor("arena", [128, words], F32)
        self.top = 0
        self.marks = []

    def alloc(self, nwords):
        nwords = (nwords + 7) // 8 * 8
        off = self.top
        self.top += nwords
        assert self.top <= self.words, f"arena overflow {self.top} > {self.words}"
        return off

    def f32(self, shape):
        n = int(np.prod(shape[1:]))
        off = self.alloc(n)
        return _reshape(self.t[0:shape[0], off:off + n], shape)

    def bf16(self, shape):
        n = int(np.prod(shape[1:]))
        assert n % 2 == 0
        off = self.alloc(n // 2)
        return _reshape(self.t[0:shape[0], off:off + n // 2].bitcast(BF16), shape)

    def i32(self, shape):
        n = int(np.prod(shape[1:]))
        off = self.alloc(n)
        return _reshape(self.t[0:shape[0], off:off + n].bitcast(I32), shape)

    def push(self):
        self.marks.append(self.top)

    def pop(self):
        self.top = self.marks.pop()


D = 2048
NT = 1024
TS = 8
TOK = NT + TS
TT = [(0, 512), (512, 512), (1024, 8)]
NCH = 9
DEPTH = 4
NA = 2
HD = 128
NH = 16
IDXD = 64
DFF = 5504
NFC = DFF // 128
A_IN = 4176
PAST = 16384
PAGE = 128
NPAGES = 128
NPHYS = 1280
WBUF = 2048
ALPHA = (2 * DEPTH) ** 0.25
LN_EPS = 1e-5
NEGV = -30000.0
B_GROUPS = ((128, 1), (512, 4), (2048, 16))
ZA = 2176
ZB = 2304
ZS = 16640
import os
KSKIP = set(os.environ.get('KSKIP', '').split(','))
SASTOP = int(os.environ.get('SASTOP', '9'))


def t5_bucket_np(dist):
    dist = np.maximum(dist, 0)
    exact = 16
    far = exact + (np.log(np.maximum(dist, 1).astype(np.float32) / exact) / math.log(2048 / exact) * (32 - exact)).astype(np.int32)
    return np.where(dist < exact, dist, np.minimum(far, 31))


def _jfar():
    b = t5_bucket_np(np.arange(0, PAST + 64))
    dsat = int(np.argmax(b == 31))
    assert (b[dsat:] == 31).all()
    return (PAST - 127 - dsat) // 128 + 1


JFAR = _jfar()


def _onehot_line(Z, valid_fn):
    z = np.arange(Z)
    d = z - 127
    ok = valid_fn(d)
    oh = np.zeros((33, Z), np.float32)
    bk = t5_bucket_np(np.maximum(d, 0))
    oh[bk[ok], z[ok]] = 1.0
    oh[32, z[~ok]] = 1.0
    return oh


def _consts():
    ohA = _onehot_line(ZA, lambda d: d >= 0)
    ohB = np.concatenate([_onehot_line(ZB, lambda d, w=w, r=r: (d >= 0) & (d <= w) & (d % r == 0)) for (w, r) in B_GROUPS], axis=1)
    ohS = _onehot_line(ZS, lambda d: d >= 0)
    q = np.arange(128)
    cmask = np.where(q[None, :] <= q[:, None], 0.0, -1e30).astype(np.float32)
    return {
        "ohA": ohA, "ohB": ohB, "ohS": ohS,
        "Jm": np.ascontiguousarray(np.eye(128, dtype=np.float32)[::-1]),
        "ident": np.eye(128, dtype=np.float32),
        "cmask": cmask,
        "iota": np.arange(128, dtype=np.float32).reshape(128, 1),
    }


class B:
    def __init__(self):
        nc = self.nc = bass.Bass("TRN2", target_bir_lowering=False)
        self.P = Prog(nc)
        self.ar = Arena(nc, 51000)
        self.ps = [nc.alloc_psum_tensor(f"ps{i}", [128, 512], F32) for i in range(8)]
        self.uid = 0
        self.evac_rr = 0
        self.cast_rr = 0

    def din(self, name, shape, dt=F32):
        return self.nc.dram_tensor(name, list(shape), dt, kind="ExternalInput")

    def dout(self, name, shape, dt=F32):
        return self.nc.dram_tensor(name, list(shape), dt, kind="ExternalOutput")

    def dint(self, name, shape, dt=F32):
        return self.nc.dram_tensor(name, list(shape), dt)

    def dma(self, out, in_, reads, writes, eng=SP, nc_ok=False):
        if nc_ok:
            def fn(e):
                with self.nc.allow_non_contiguous_dma(reason="small strided"):
                    return e.dma_start(out=out, in_=in_)
        else:
            def fn(e):
                return e.dma_start(out=out, in_=in_)
        return self.P.op(eng, fn, reads, writes, dma=True)

    def mm(self, out, lhsT, rhs, start, stop, reads, writes):
        return self.P.op(PE, lambda e: e.matmul(out, lhsT=lhsT, rhs=rhs, start=start, stop=stop), reads, writes)

    def act(self, out, in_, func, reads, writes, scale=1.0, bias=0.0):
        return self.P.op(ACT, lambda e: e.activation(out=out, in_=in_, func=func, scale=scale, bias=bias), reads, writes)

    def copy(self, eng, out, in_, reads, writes):
        if eng == ACT:
            return self.P.op(ACT, lambda e: e.activation(out=out, in_=in_, func=ACTF.Copy), reads, writes)
        return self.P.op(eng, lambda e: e.tensor_copy(out=out, in_=in_), reads, writes)

    def tt(self, eng, out, in0, in1, op, reads, writes):
        return self.P.op(eng, lambda e: e.tensor_tensor(out=out, in0=in0, in1=in1, op=op), reads, writes)

    def ts(self, eng, out, in0, s1, s2, op0, op1, reads, writes):
        if op1 is None:
            return self.P.op(eng, lambda e: e.tensor_scalar(out=out, in0=in0, scalar1=s1, scalar2=None, op0=op0), reads, writes)
        return self.P.op(eng, lambda e: e.tensor_scalar(out=out, in0=in0, scalar1=s1, scalar2=s2, op0=op0, op1=op1), reads, writes)

    def stt(self, eng, out, in0, scalar, in1, op0, op1, reads, writes):
        return self.P.op(eng, lambda e: e.scalar_tensor_tensor(out=out, in0=in0, scalar=scalar, in1=in1, op0=op0, op1=op1), reads, writes)

    def evac_eng(self):
        self.evac_rr += 1
        return ACT if self.evac_rr % 2 else DVE

    def wstream_init(self, kch_max, nstage=3, nbf=3):
        ar = self.ar
        self.ws_stage = [ar.f32([128, kch_max, 128]) for _ in range(nstage)]
        self.ws_bf = [ar.bf16([128, kch_max, 128]) for _ in range(nbf)]
        self.ws_q = []
        self.ws_issued = 0
        self.ws_cast = 0

    def wstream_set(self, slabs):
        self.ws_q = list(slabs)
        self.ws_base = self.ws_issued
        assert self.ws_issued == self.ws_cast

    def _ws_issue(self, i):
        t, row0, kch, col0, ncols = self.ws_q[i]
        n = self.ws_base + i
        st = self.ws_stage[n % len(self.ws_stage)]
        src = t.ap()[row0:row0 + kch * 128, col0:col0 + ncols].rearrange("(k p) c -> p k c", p=128)
        self.dma(st[:, 0:kch, 0:ncols], src, reads=[], writes=[("wst", n % len(self.ws_stage))])
        self.ws_issued += 1

    def wget(self, i, depth=2):
        while self.ws_issued - self.ws_base < min(len(self.ws_q), i + 1 + depth):
            self._ws_issue(self.ws_issued - self.ws_base)
        assert self.ws_cast - self.ws_base == i, (self.ws_cast, self.ws_base, i)
        t, row0, kch, col0, ncols = self.ws_q[i]
        n = self.ws_base + i
        si = n % len(self.ws_stage)
        bi = n % len(self.ws_bf)
        st = self.ws_stage[si]
        bf = self.ws_bf[bi]
        self.cast_rr += 1
        eng = POOL
        self.copy(eng, bf[:, 0:kch, 0:ncols], st[:, 0:kch, 0:ncols], reads=[("wst", si)], writes=[("wbf", bi)])
        self.ws_cast += 1
        return bf, ("wbf", bi)

    def build(self):
        nc, P, ar = self.nc, self.P, self.ar
        xT = self.din("xT", [D, TOK])
        stT = self.din("stT", [DEPTH, DFF, 2])
        w_in = self.din("a_w_in", [NA * D, A_IN])
        w_oa = self.din("a_w_o", [NA * D, D])
        kn_g = self.din("a_kn_g", [NA, IDXD])
        kn_b = self.din("a_kn_b", [NA, IDXD])
        w_kv = self.din("b_w_kv", [D, 1024])
        w_q = self.din("b_w_q", [2 * D, 6144])
        w_ob = self.din("b_w_o", [2 * D, D])
        w_up = self.din("ffn_w_up", [DEPTH * D, 2 * DFF])
        cw = self.din("ffn_conv_w", [DEPTH, 3, DFF])
        cb = self.din("ffn_conv_b", [DEPTH, DFF])
        w_dn = self.din("ffn_w_down", [DEPTH * DFF, D])
        ln_g = self.din("ln_g", [DEPTH * 2, D])
        ln_b = self.din("ln_b", [DEPTH * 2, D])
        hflag = self.din("hflag", [128, 1])
        pmask_d = self.din("pmask", [128, 1])
        relb = self.din("rel_bias", [32, 16])
        ohA = self.din("ohA", [33, ZA])
        ohB = self.din("ohB", [33, 3 * ZB])
        ohS = self.din("ohS", [33, ZS])
        Jm_d = self.din("Jm", [128, 128])
        ident_d = self.din("ident", [128, 128])
        cmask_d = self.din("cmask", [128, 128])
        iota_d = self.din("iota", [128, 1])
        ptab = self.din("ptab", [1, NPAGES], I32)
        if "sa" in KSKIP:
            ck_a = cv_a = cki_a = [None, None]
        else:
            ck_a = [self.din(f"cache_k_a{a}", [NPHYS * PAGE, 512]) for a in range(NA)]
            cv_a = [self.din(f"cache_v_a{a}", [NPHYS * PAGE, 512]) for a in range(NA)]
            cki_a = [self.din(f"cache_kidx_a{a}", [NPHYS * PAGE, IDXD]) for a in range(NA)]
        ck_b = self.din("ckb", [WBUF, 512])
        cv_b = self.din("cvb", [WBUF, 512])
        self.cache = (ck_a, cv_a, cki_a, ck_b, cv_b, ptab)
        yT = self.dout("yT", [D, TOK])
        o_k = self.dout("o_k", [NA, TOK, 512])
        o_v = self.dout("o_v", [NA, TOK, 512])
        o_ki = self.dout("o_ki", [NA, TOK, IDXD])
        o_kb = self.dout("o_kb", [TOK, 512])
        o_vb = self.dout("o_vb", [TOK, 512])
        o_ffp = self.dout("o_ffp", [DEPTH, 128, NFC * 2])
        o_ffs = self.dout("o_ffs", [DEPTH, 128, NFC * 2])
        h32_d = self.dint("h32_d", [16, 128, TOK])
        xpre_d = self.dint("xpre_d", [16, 128, TOK])
        snd_g = self.dint("snd_g", [128, NFC * 2])
        rcv_g = self.dint("rcv_g", [256, NFC * 2])
        self.PAIRS = [[0, 1], [2, 3], [4, 5], [6, 7]]
        self.OT_d = self.dint("OT_d", [16, 128, TOK], BF16)
        self.snd_a = self.dint("snd_a", [1024, 1024], BF16)
        self.rcv_a = self.dint("rcv_a", [2048, 1024], BF16)
        self.snd_k = self.dint("snd_k", [128, 1024], BF16)
        self.rcv_k = self.dint("rcv_k", [256, 1024], BF16)
        self.snd_b = self.dint("snd_b", [1024, 1024], BF16)
        self.rcv_b = self.dint("rcv_b", [2048, 1024], BF16)
        self.LA_d = self.dint("LA_d", [16, ZA])
        self.LB_d = self.dint("LB_d", [16, 3 * ZB])
        self.LS_d = self.dint("LS_d", [16, ZS])
        self.scr_d = self.dint("scr_d", [128, 256])
        self.scr2_d = self.dint("scr2_d", [8, 16])
        self.scr3_d = self.dint("scr3_d", [8, 1])
        self.h32_d, self.xpre_d = h32_d, xpre_d

        self.hT_off = ar.top
        hT = ar.bf16([128, 16, TOK])
        self.hT_words = ar.top - self.hT_off
        self.KTbS = ar.bf16([128, 4, TS])
        self.VbS = ar.bf16([TS, 512])
        onesF = ar.f32([128, 128])
        lng = ar.f32([128, 8, 16])
        lnb = ar.f32([128, 8, 16])
        hfl = ar.f32([128, 1])
        P.op(DVE, lambda e: e.memset(onesF, 1.0), [], ["onesF"])
        self.epsc = ar.f32([128, 1])
        P.op(DVE, lambda e: e.memset(self.epsc, LN_EPS), [], ["epsc"])
        self.dma(lng, ln_g.ap().rearrange("l (c p) -> p l c", p=128), [], ["lng"], nc_ok=True)
        self.dma(lnb, ln_b.ap().rearrange("l (c p) -> p l c", p=128), [], ["lnb"], nc_ok=True)
        self.dma(hfl, hflag.ap(), [], ["hfl"])
        self.Jm = ar.f32([128, 128]); self.dma(self.Jm, Jm_d.ap(), [], ["Jm"])
        identf = ar.f32([128, 128]); self.dma(identf, ident_d.ap(), [], ["identf"])
        self.identb = ar.bf16([128, 128]); self.copy(DVE, self.identb, identf, ["identf"], ["identb"])
        self.onesB = ar.bf16([128, 128]); P.op(DVE, lambda e: e.memset(self.onesB, 1.0), [], ["onesB"])
        self.cmask = ar.f32([128, 128]); self.dma(self.cmask, cmask_d.ap(), [], ["cmask"])
        self.pmask = ar.f32([128, 1]); self.dma(self.pmask, pmask_d.ap(), [], ["pmask"])
        self.iota = ar.f32([128, 1]); self.dma(self.iota, iota_d.ap(), [], ["iota"])
        self.n1e29 = ar.f32([128, 1]); P.op(DVE, lambda e: e.memset(self.n1e29, -1e29), [], ["n1e29"])
        self.onesF = onesF
        self.maskB = ar.bf16([128, 128])
        P.op(DVE, lambda e: e.memset(self.maskB, 1.0), [], ["maskB"])
        self.ts(DVE, self.maskB, self.maskB, hfl[:, 0:1], None, ALU.mult, None, ["maskB", "hfl"], ["maskB"])
        ar.push()
        RB = ar.f32([33, 16])
        self.dma(RB[0:32, :], relb.ap(), [], ["RB"])
        P.op(DVE, lambda e: e.memset(RB[32:33, :], NEGV), [], ["RB2"])
        ohs = [ar.f32([33, 512]) for _ in range(2)]
        lns = [ar.f32([16, 512]) for _ in range(2)]
        k = 0
        for (oh, ld, Z) in ((ohA, self.LA_d, ZA), (ohB, self.LB_d, 3 * ZB), (ohS, self.LS_d, ZS)):
            if "il" in KSKIP:
                continue
            for z0 in range(0, Z, 512):
                n = min(512, Z - z0)
                o, l = ohs[k % 2], lns[k % 2]
                self.dma(o[:, 0:n], oh.ap()[:, z0:z0 + n], [], [("ohs", k % 2)])
                self.mm(self.ps[k % 2][0:16, 0:n], RB, o[:, 0:n], True, True, ["RB", "RB2", ("ohs", k % 2)], [f"ps{k % 2}"])
                self.copy(self.evac_eng(), l[:, 0:n], self.ps[k % 2][0:16, 0:n], [f"ps{k % 2}"], [("lns", k % 2)])
                self.dma(ld.ap()[:, z0:z0 + n], l[:, 0:n], [("lns", k % 2)], [("line", id(ld), z0)])
                k += 1
        ar.pop()

        P.barrier()
        ar.push()
        xs = [ar.f32([128, TOK]) for _ in range(2)]
        for cc in range(16):
            x = xs[cc % 2]
            self.dma(x, xT.ap()[cc * 128:(cc + 1) * 128, :], [], [("xs", cc % 2)])
            self.copy(self.evac_eng(), hT[:, cc, :], x, [("xs", cc % 2)], [("hT", cc)])
            self.dma(h32_d.ap()[cc], x, [("xs", cc % 2)], [("h32", cc)])
        ar.pop()
        P.barrier()

        for layer in range(DEPTH):
            if layer < NA:
                self.a_layer(layer, hT, w_in, kn_g, kn_b, o_k, o_v, o_ki)
                self.proj_residual(w_oa, layer * D)
            else:
                if layer == NA:
                    self.shared_kv(hT, w_kv, o_kb, o_vb)
                if "b" not in KSKIP:
                    self.b_layer(layer - NA, hT, w_q)
                self.proj_residual(w_ob, (layer - NA) * D)
            self.layer_norm(xpre_d, h32_d, hT, lng, lnb, layer * 2, onesF)
            self.ffn(layer, hT, w_up, cw, cb, w_dn, stT, h32_d, xpre_d, o_ffp, o_ffs, snd_g, rcv_g, hfl)
            self.layer_norm(xpre_d, h32_d, hT, lng, lnb, layer * 2 + 1, onesF)
        ar.push()
        xs = [ar.f32([128, TOK]) for _ in range(2)]
        for cc in range(16):
            x = xs[cc % 2]
            self.dma(x, h32_d.ap()[cc], [("h32", cc)], [("xs", cc % 2)])
            self.dma(yT.ap()[cc * 128:(cc + 1) * 128, :], x, [("xs", cc % 2)], [("yT", cc)])
        ar.pop()
        P.emit()
        return nc

    def mix_residual_zero(self, h32_d, xpre_d):
        ar, P = self.ar, self.P
        ar.push()
        xs = [ar.f32([128, TOK]) for _ in range(2)]
        for cc in range(16):
            x = xs[cc % 2]
            self.dma(x, h32_d.ap()[cc], [("h32", cc)], [("xs", cc % 2)])
            self.ts(DVE, x, x, ALPHA, None, ALU.mult, None, [("xs", cc % 2)], [("xs", cc % 2)])
            self.dma(xpre_d.ap()[cc], x, [("xs", cc % 2)], [("xpre", cc)])
        ar.pop()
        P.barrier()

    def layer_norm(self, xpre_d, h32_d, hT, lng, lnb, li, onesF):
        ar, P, ps = self.ar, self.P, self.ps
        ar.push()
        X = ar.f32([128, 16, TOK])
        sq = [ar.f32([128, 512]) for _ in range(2)]
        M = ar.f32([128, TOK])
        R = ar.f32([128, TOK])
        for cc in range(16):
            self.dma(X[:, cc, :], xpre_d.ap()[cc], [("xpre", cc)], [("X", cc)])
        k = 0
        for ti, (t0, tn) in enumerate(TT):
            p1, p2 = ps[2 * (ti % 2)], ps[2 * (ti % 2) + 1]
            r1, r2 = f"ps{2 * (ti % 2)}", f"ps{2 * (ti % 2) + 1}"
            for cc in range(16):
                s = sq[k % 2]
                self.act(s[:, 0:tn], X[:, cc, t0:t0 + tn], ACTF.Square, [("X", cc)], [("sq", k % 2)])
                self.mm(p1[:, 0:tn], onesF, X[:, cc, t0:t0 + tn], cc == 0, cc == 15, [("X", cc), "onesF"], [r1])
                self.mm(p2[:, 0:tn], onesF, s[:, 0:tn], cc == 0, cc == 15, [("sq", k % 2), "onesF"], [r2])
                k += 1
            self.ts(DVE, M[:, t0:t0 + tn], p1[:, 0:tn], 1.0 / D, None, ALU.mult, None, [r1], [("M", ti)])
            self.ts(DVE, R[:, t0:t0 + tn], p2[:, 0:tn], 1.0 / D, None, ALU.mult, None, [r2], [("R", ti)])
            self.tt(POOL, sq[0][:, 0:tn], M[:, t0:t0 + tn], M[:, t0:t0 + tn], ALU.mult, [("M", ti)], [("sq", 0)])
            self.tt(DVE, R[:, t0:t0 + tn], R[:, t0:t0 + tn], sq[0][:, 0:tn], ALU.subtract, [("R", ti), ("sq", 0)], [("R", ti)])
            self.act(R[:, t0:t0 + tn], R[:, t0:t0 + tn], ACTF.Sqrt, [("R", ti), "epsc"], [("R", ti)], bias=self.epsc[:, 0:1])
            P.op(DVE, lambda e, t0=t0, tn=tn: e.reciprocal(out=R[:, t0:t0 + tn], in_=R[:, t0:t0 + tn]), [("R", ti)], [("R", ti)])
        for cc in range(16):
            for ti, (t0, tn) in enumerate(TT):
                e1 = DVE if (cc + ti) % 2 else POOL
                self.tt(e1, X[:, cc, t0:t0 + tn], X[:, cc, t0:t0 + tn], M[:, t0:t0 + tn], ALU.subtract, [("X", cc), ("M", ti)], [("X", cc)])
                self.tt(e1, X[:, cc, t0:t0 + tn], X[:, cc, t0:t0 + tn], R[:, t0:t0 + tn], ALU.mult, [("X", cc), ("R", ti)], [("X", cc)])
            self.act(X[:, cc, :], X[:, cc, :], ACTF.Identity, [("X", cc), "lng", "lnb"], [("X", cc)],
                     scale=lng[:, li, cc:cc + 1], bias=lnb[:, li, cc:cc + 1])
            self.copy(DVE, hT[:, cc, :], X[:, cc, :], [("X", cc)], [("hT", cc)])
            self.dma(h32_d.ap()[cc], X[:, cc, :], [("X", cc)], [("h32", cc)])
        ar.pop()
        P.barrier()

    def alias_hT_bf16(self, shape):
        n = int(np.prod(shape[1:]))
        assert n // 2 <= self.hT_words
        ap = self.ar.t[0:shape[0], self.hT_off:self.hT_off + n // 2].bitcast(BF16)
        return _reshape(ap, shape)

    def transpose_to(self, dst, src, nrows, k, sres, dres):
        ps = self.ps
        ncols = src.shape[-1]
        bank = 6 + k % 2
        tp = ps[bank][:, 0:64].bitcast(BF16)
        self.P.op(PE, lambda e: e.transpose(out=tp[0:ncols, 0:nrows], in_=src, identity=self.identb[0:nrows, 0:nrows]),
                  list(sres) + ["identb"], [f"ps{bank}"])
        self.copy(ACT, dst, tp[0:ncols, 0:nrows], [f"ps{bank}"], dres)

    def a_layer(self, layer, hT, w_in, kn_g, kn_b, o_k, o_v, o_ki):
        ar, P = self.ar, self.P
        ar.push()
        QT = ar.bf16([128, 16, TOK])
        KTo = ar.bf16([128, 4, TOK])
        Vb = ar.bf16([128, NCH, 512])
        QITs = ar.bf16([128, 8, TS])
        KITn = ar.bf16([128, TS])
        WIsS = ar.f32([TS, 16])
        ar.push()
        KTall = ar.bf16([128, 4, 2048])
        Vall = ar.bf16([128, 16, 512])
        ar.push()
        QIT = ar.bf16([128, 8, TOK])
        KITo = ar.bf16([128, TOK])
        WIs = ar.f32([128, NCH, 16])
        self.a_inproj(layer, hT, w_in, kn_g, kn_b, o_k, o_v, o_ki, QT, KTo, Vb, QIT, KITo, WIs)
        self.copy(DVE, QITs, QIT[:, :, NT:TOK], [], ["QITs"])
        self.copy(DVE, KITn, KITo[:, NT:TOK], [], ["KITn"])
        self.copy(DVE, WIsS, WIs[0:TS, 8, :], [], ["WIsS"])
        P.barrier()
        if "cx" not in KSKIP:
            self.a_exchange(KTo, Vb, KITo, KTall, Vall)
        selT = self.alias_hT_bf16([128, 8, 16, 128])
        if "ix" not in KSKIP:
            self.a_indexer(QIT, KITo, WIs, selT)
        ar.pop()
        P.barrier()
        if "pa" not in KSKIP:
            self.attn_prompt_a(QT, KTall, Vall, selT)
        ar.pop()
        P.barrier()
        if "sa" not in KSKIP:
            self.attn_sample_a(layer, QT, KTo, Vb, QITs, KITn, WIsS)
        ar.pop()
        P.barrier()

    def a_inproj(self, layer, hT, w_in, kn_g, kn_b, o_k, o_v, o_ki, QT, KTo, Vb, QIT, KITo, WIs):
        ar, P, ps = self.ar, self.P, self.ps
        ar.push()
        self.wstream_init(16)
        tmp4 = [ar.f32([128, 128]) for _ in range(4)]
        KW = ar.f32([128, NCH, 128])
        gb = ar.f32([128, 2, IDXD])
        self.dma(gb[:, 0, :], kn_g.ap()[layer:layer + 1, :].partition_broadcast(128), [], ["gb0"])
        self.dma(gb[:, 1, :], kn_b.ap()[layer:layer + 1, :].partition_broadcast(128), [], ["gb1"])
        slabs = []
        for c in range(33):
            c0 = c * 128
            slabs.append((w_in, layer * D, 16, c0, min(128, A_IN - c0)))
        self.wstream_set(slabs)
        hreads = [("hT", kc) for kc in range(16)]
        k4 = 0
        for c in range(33):
            ncols = slabs[c][4]
            Wb, wres = self.wget(c)
            ws_dst = None
            if c < 16:
                ws_dst, scale = QT[:, c, :], HD ** -0.5
            elif c < 20:
                ws_dst, scale = KTo[:, c - 16, :], 1.0
            elif 24 <= c < 32:
                ws_dst, scale = QIT[:, c - 24, :], IDXD ** -0.5
            if ws_dst is not None:
                for ti, (t0, tn) in enumerate(TT):
                    pb, pr = ps[3 + ti], f"ps{3 + ti}"
                    for kc in range(16):
                        self.mm(pb[:, 0:tn], Wb[:, kc, :], hT[:, kc, t0:t0 + tn], kc == 0, kc == 15, hreads + [wres], [pr])
                    self.act(ws_dst[:, t0:t0 + tn], pb[:, 0:tn], ACTF.Copy, [pr], [("ws", c, ti)], scale=scale)
            if 16 <= c < 24 or c == 32:
                for ch in range(NCH):
                    ntk = 128 if ch < 8 else TS
                    pb, pr = ps[ch % 3], f"ps{ch % 3}"
                    for kc in range(16):
                        self.mm(pb[0:ntk, 0:ncols], hT[:, kc, ch * 128:ch * 128 + ntk], Wb[:, kc, 0:ncols], kc == 0, kc == 15,
                                hreads + [wres], [pr])
                    if c == 32:
                        self.copy(self.evac_eng(), KW[0:ntk, ch, 0:ncols], pb[0:ntk, 0:ncols], [pr], [("KW", ch)])
                    else:
                        t4 = tmp4[k4 % 4]
                        r4 = ("tmp4", k4 % 4)
                        k4 += 1
                        self.copy(ACT, t4[0:ntk, :], pb[0:ntk, 0:128], [pr], [r4])
                        j = (c - 16) % 4
                        od = o_k if c < 20 else o_v
                        self.dma(od.ap()[layer, ch * 128:ch * 128 + ntk, j * 128:(j + 1) * 128], t4[0:ntk, :], [r4], [("okv", c, ch)])
                        if c >= 20:
                            self.copy(DVE, Vb[0:ntk, ch, j * 128:(j + 1) * 128], t4[0:ntk, :], [r4], [("Vb", ch, j)])
        for ch in range(NCH):
            ntk = 128 if ch < 8 else TS
            self.ts(DVE, WIs[0:ntk, ch, :], KW[0:ntk, ch, 64:80], NH ** -0.5, None, ALU.mult, None, [("KW", ch)], [("WIs", ch)])
        st6 = ar.f32([128, NCH, 6])
        mv = ar.f32([128, NCH, 2])
        KIn = ar.f32([128, NCH, IDXD])
        KId = ar.bf16([128, NCH, 128])
        for ch in range(NCH):
            ntk = 128 if ch < 8 else TS
            r = [("KW", ch)]
            P.op(DVE, lambda e, ch=ch, ntk=ntk: e.bn_stats(out=st6[0:ntk, ch, :], in_=KW[0:ntk, ch, 0:IDXD]), r, [("st6", ch)])
            P.op(DVE, lambda e, ch=ch, ntk=ntk: e.bn_aggr(out=mv[0:ntk, ch, :], in_=st6[0:ntk, ch, :]), [("st6", ch)], [("mv", ch)])
            self.act(mv[0:ntk, ch, 1:2], mv[0:ntk, ch, 1:2], ACTF.Sqrt, [("mv", ch), "epsc"], [("mv", ch)], bias=self.epsc[0:ntk, 0:1])
            P.op(DVE, lambda e, ch=ch, ntk=ntk: e.reciprocal(out=mv[0:ntk, ch, 1:2], in_=mv[0:ntk, ch, 1:2]), [("mv", ch)], [("mv", ch)])
            self.ts(DVE, KIn[0:ntk, ch, :], KW[0:ntk, ch, 0:IDXD], mv[0:ntk, ch, 0:1], mv[0:ntk, ch, 1:2], ALU.subtract, ALU.mult,
                    r + [("mv", ch)], [("KIn", ch)])
            self.tt(DVE, KIn[0:ntk, ch, :], KIn[0:ntk, ch, :], gb[0:ntk, 0, :], ALU.mult, [("KIn", ch), "gb0"], [("KIn", ch)])
            self.tt(DVE, KIn[0:ntk, ch, :], KIn[0:ntk, ch, :], gb[0:ntk, 1, :], ALU.add, [("KIn", ch), "gb1"], [("KIn", ch)])
            self.dma(o_ki.ap()[layer, ch * 128:ch * 128 + ntk, :], KIn[0:ntk, ch, :], [("KIn", ch)], [("o_ki", ch)])
            self.copy(DVE, KId[0:ntk, ch, 0:64], KIn[0:ntk, ch, :], [("KIn", ch)], [("KId", ch)])
            self.copy(POOL, KId[0:ntk, ch, 64:128], KIn[0:ntk, ch, :], [("KIn", ch)], [("KId2", ch)])
            self.transpose_to(KITo[:, ch * 128:ch * 128 + ntk], KId[0:ntk, ch, :], ntk, ch, [("KId", ch), ("KId2", ch)], [("KITo", ch)])
        ar.pop()
        P.barrier()

    def a_exchange(self, KTo, Vb, KITo, KTall, Vall):
        P = self.P
        snd, rcv = self.snd_a, self.rcv_a
        parts = []
        for g in range(4):
            self.dma(snd.ap()[g * 128:(g + 1) * 128, :], KTo[:, g, 0:NT], [], [("snd_a", "k", g)], eng=POOL)
            parts.append(("snd_a", "k", g))
        vview = snd.ap()[512:1024, :].rearrange("r (two c) -> (r two) c", two=2)
        for ch in range(8):
            self.dma(vview[ch * 128:(ch + 1) * 128, :], Vb[:, ch, :], [], [("snd_a", "v", ch)], eng=POOL)
            parts.append(("snd_a", "v", ch))
        self.dma(self.snd_k.ap(), KITo[:, 0:NT], [], [("snd_a", "ki")], eng=POOL)
        P.op(POOL, lambda e: e.collective_compute("AllGather", ALU.bypass, replica_groups=self.PAIRS,
                                                  ins=[snd.ap().opt()], outs=[rcv.ap().opt()]), parts, ["rcv_a"])
        P.op(POOL, lambda e: e.collective_compute("AllGather", ALU.bypass, replica_groups=self.PAIRS,
                                                  ins=[self.snd_k.ap().opt()], outs=[self.rcv_k.ap().opt()]), [("snd_a", "ki")], ["rcv_k"])
        rv = rcv.ap()[512:1024, :].rearrange("r (two c) -> (r two) c", two=2)
        for g in range(4):
            self.dma(KTall[:, g, 0:NT], rcv.ap()[g * 128:(g + 1) * 128, :], ["rcv_a"], [("KTall", g, 0)])
            self.dma(KTall[:, g, NT:2 * NT], snd.ap()[g * 128:(g + 1) * 128, :], parts, [("KTall", g, 1)])
        for ch in range(8):
            self.dma(Vall[:, ch, :], rv[ch * 128:(ch + 1) * 128, :], ["rcv_a"], [("Vall", ch)])
            self.dma(Vall[:, 8 + ch, :], vview[ch * 128:(ch + 1) * 128, :], parts, [("Vall", 8 + ch)])

    def a_indexer(self, QIT, KITo, WIs, selT):
        ar, P, ps = self.ar, self.P, self.ps
        rcv, snd = self.rcv_a, self.snd_a
        KITall = ar.bf16([128, 2048])
        for hf in range(2):
            self.dma(KITall[hf * 64:(hf + 1) * 64, 0:NT], self.rcv_k.ap()[hf * 64:(hf + 1) * 64, :], ["rcv_k"], [("KITall", hf, 0)])
            self.dma(KITall[hf * 64:(hf + 1) * 64, NT:2 * NT], self.snd_k.ap()[hf * 64:(hf + 1) * 64, :], [("snd_a", "ki")], [("KITall", hf, 1)])
        kall = [("KITall", a, b) for a in range(2) for b in range(2)]
        I = ar.f32([128, 2048])
        Wk = ar.f32([128, 2048])
        S01 = ar.bf16([128, 2048])
        Rl = [ar.f32([128, 512]) for _ in range(2)]
        m8 = ar.f32([128, 32, 8])
        thr2 = ar.f32([128, 1])
        k = 0
        for i in range(8):
            ncol = NT + 128 * (i + 1)
            nt = (ncol + 511) // 512
            for h in range(16):
                hp = (h % 2) * 64
                w = WIs[:, i, h:h + 1]
                for kt in range(nt):
                    n = min(512, ncol - kt * 512)
                    pb, pr = ps[k % 2], f"ps{k % 2}"
                    rl, rr = Rl[k % 2], ("Rl", k % 2)
                    k += 1
                    self.mm(pb[:, 0:n], QIT[hp:hp + 64, h // 2, i * 128:(i + 1) * 128], KITall[hp:hp + 64, kt * 512:kt * 512 + n],
                            True, True, kall, [pr])
                    self.act(rl[:, 0:n], pb[:, 0:n], ACTF.Relu, [pr], [rr])
                    Ic = I[:, kt * 512:kt * 512 + n]
                    if h == 0:
                        self.ts(DVE, Ic, rl[:, 0:n], w, None, ALU.mult, None, [rr], [("I", kt)])
                    else:
                        self.stt(DVE, Ic, rl[:, 0:n], w, Ic, ALU.mult, ALU.add, [rr, ("I", kt)], [("I", kt)])
            Iall = [("I", kt) for kt in range(4)]
            self.ts(DVE, I[:, 0:NT], I[:, 0:NT], self.pmask[:, 0:1], None, ALU.add, None, Iall + ["pmask"], Iall)
            dc = NT + 128 * i
            self.tt(DVE, I[:, dc:dc + 128], I[:, dc:dc + 128], self.cmask, ALU.add, Iall + ["cmask"], Iall)
            src = I
            sres = Iall
            for r in range(32):
                P.op(DVE, lambda e, r=r, src=src, ncol=ncol: e.max(out=m8[:, r, :], in_=src[:, 0:ncol]), sres, [("m8", r)])
                if r < 31:
                    P.op(DVE, lambda e, r=r, src=src, ncol=ncol: e.match_replace(out=Wk[:, 0:ncol], in_to_replace=m8[:, r, :],
                                                                                    in_values=src[:, 0:ncol], imm_value=-1e30),
                         list(sres) + [("m8", r)], ["Wk"])
                    src = Wk
                    sres = ["Wk"]
            self.ts(DVE, thr2, m8[:, 31, 7:8], -1e29, None, ALU.max, None, [("m8", 31)], ["thr2"])
            self.ts(DVE, S01[:, 0:ncol], I[:, 0:ncol], thr2[:, 0:1], None, ALU.is_ge, None, Iall + ["thr2"], ["S01"])
            for kc in range(9 + i):
                self.transpose_to(selT[:, i, kc, :], S01[:, kc * 128:(kc + 1) * 128], 128, kc, ["S01"], [("selT", i, kc)])

    def build_table(self, dst, dres, line_d, Zrow, row0, nh, z0, W, Hst, tag):
        ps = self.ps
        k = 0
        for x0 in range(0, W, 512):
            n = min(512, W - x0)
            src = bass.AP(line_d, row0 * Zrow + z0 + x0, [[1, 128], [Zrow, nh], [1, n]])
            self.dma(Hst[:, 0:nh, 0:n], src, [], [("Hst", tag)], nc_ok=(n < 128))
            if nh * n <= 512:
                bank = 7
                k += 1
                self.mm(ps[bank][:, 0:nh * n], self.Jm, Hst[:, 0:nh, 0:n], True, True, [("Hst", tag), "Jm"], [f"ps{bank}"])
                self.copy(ACT, dst[:, :, x0:x0 + n], ps[bank][:, 0:nh * n].rearrange("p (a b) -> p a b", a=nh), [f"ps{bank}"], [dres])
            else:
                for r in range(nh):
                    bank = 7
                    k += 1
                    self.mm(ps[bank][:, 0:n], self.Jm, Hst[:, r, 0:n], True, True, [("Hst", tag), "Jm"], [f"ps{bank}"])
                    self.copy(ACT, dst[:, r, x0:x0 + n], ps[bank][:, 0:n], [f"ps{bank}"], [dres])

    def attn_stage1(self, k, t):
        ps = self.ps
        Tm, Pe, Pm = t["bufs"]
        a, b = t["shp"]
        n = a * b
        nkeys = t["nkeys"]
        reads = t["reads"]
        r3 = lambda ap: ap.rearrange("p (a b) -> p a b", a=a)
        LB = (0, 1, 6)
        bl = LB[k % 3]
        pl, plr = ps[bl], f"ps{bl}"
        tm, pe, pm = Tm[k % 3], Pe[k % 3], Pm[k % 3]
        self.mm(pl[0:nkeys, 0:n], t["lhsT_k"], t["rhs_q"], True, True, reads, [plr])
        self.tt(DVE, r3(tm[0:nkeys, 0:n]), r3(pl[0:nkeys, 0:n]), t["bias"], ALU.add, [plr] + reads, [("Tm", k % 3)])
        self.act(pe[0:nkeys, 0:n], tm[0:nkeys, 0:n], ACTF.Exp, [("Tm", k % 3)], [("Pe", k % 3)])
        if t["mask"] is not None:
            self.tt(POOL, r3(pm[0:nkeys, 0:n]), r3(pe[0:nkeys, 0:n]), t["mask"], ALU.mult, [("Pe", k % 3)] + reads, [("Pm", k % 3)])
            t["src"], t["sr"] = pm, ("Pm", k % 3)
        else:
            t["src"], t["sr"] = pe, ("Pe", k % 3)

    def attn_stage2(self, k, t):
        ps = self.ps
        a, b = t["shp"]
        n = a * b
        nkeys = t["nkeys"]
        bank_o = t["bank_o"]
        po, por = ps[bank_o], f"ps{bank_o}"
        pS, pSr = ps[bank_o + 2], f"ps{bank_o + 2}"
        src, sr = t["src"], t["sr"]
        self.mm(po[:, 0:n], t["v_lhsT"], src[0:nkeys, 0:n], t["first"], t["last"], [sr] + t["reads"], [por])
        self.mm(pS[:, 0:n], self.onesB[0:nkeys, :], src[0:nkeys, 0:n], t["first"], t["last"], [sr, "onesB"], [pSr])

    def attn_run(self, tiles, k0, LA=2):
        n = len(tiles)
        for j in range(n + LA):
            if j < n:
                self.attn_stage1(k0 + j, tiles[j])
            jj = j - LA
            if jj >= 0:
                t = tiles[jj]
                self.attn_stage2(k0 + jj, t)
                if t["last"]:
                    t["finish"]()
        return k0 + n

    def attn_finish(self, gi, shp, bank_o, rcb, Ob, dst_ap):
        ps = self.ps
        a, b = shp
        n = a * b
        po, por = ps[bank_o], f"ps{bank_o}"
        pS, pSr = ps[bank_o + 2], f"ps{bank_o + 2}"
        rc, ob = rcb[gi % 2], Ob[gi % 2]
        self.P.op(DVE, lambda e: e.reciprocal(out=rc[:, 0:n], in_=pS[:, 0:n]), [pSr], [("rc", gi % 2)])
        self.tt(DVE, ob[:, 0:n], po[:, 0:n], rc[:, 0:n], ALU.mult, [por, ("rc", gi % 2)], [("Ob", gi % 2)])
        self.dma(dst_ap, ob[:, 0:n].rearrange("p (a b) -> p a b", a=a), [("Ob", gi % 2)], [("OT_d", self.uid)], nc_ok=True)
        self.uid += 1

    def attn_bufs(self):
        ar = self.ar
        Tm = [ar.f32([128, 512]) for _ in range(3)]
        Pe = [ar.bf16([128, 512]) for _ in range(3)]
        Pm = [ar.bf16([128, 512]) for _ in range(3)]
        rcb = [ar.f32([128, 512]) for _ in range(2)]
        Ob = [ar.bf16([128, 512]) for _ in range(2)]
        return (Tm, Pe, Pm), rcb, Ob

    def attn_prompt_a(self, QT, KTall, Vall, selT):
        ar, P = self.ar, self.P
        ar.push()
        TB = ar.f32([128, 4, 2048])
        Hst = ar.f32([128, 4, 512])
        bufs, rcb, Ob = self.attn_bufs()
        k = 0
        gi = 0
        for g in range(4):
            self.build_table(TB, "TB", self.LA_d, ZA, 4 * g, 4, 0, 2048, Hst, "a")
            tiles = []
            for i in range(8):
                nkc = 9 + i
                bank_o = 2 + gi % 2
                dst = self.OT_d.ap()[4 * g:4 * g + 4, :, i * 128:(i + 1) * 128].rearrange("h p t -> p h t")
                fin = (lambda gi=gi, bank_o=bank_o, dst=dst: self.attn_finish(gi, (4, 128), bank_o, rcb, Ob, dst))
                for kc in range(nkc):
                    off = 8 + i - kc
                    tiles.append(dict(first=kc == 0, last=kc == nkc - 1, lhsT_k=KTall[:, g, kc * 128:(kc + 1) * 128],
                                      rhs_q=QT[:, 4 * g:4 * g + 4, i * 128:(i + 1) * 128], shp=(4, 128),
                                      bias=TB[:, :, off * 128:(off + 1) * 128],
                                      mask=selT[:, i, kc, :].unsqueeze(1).to_broadcast([128, 4, 128]),
                                      v_lhsT=Vall[:, kc, g * 128:(g + 1) * 128], nkeys=128, bufs=bufs, bank_o=bank_o,
                                      reads=["TB"], finish=fin))
                gi += 1
            k = self.attn_run(tiles, k)
        ar.pop()

    def attn_sample_a(self, layer, QT, KTo, Vb, QITs, KITn, WIsS):
        ar, P, ps = self.ar, self.P, self.ps
        ck_a, cv_a, cki_a, _, _, ptab = self.cache
        ck_a, cv_a, cki_a = ck_a[layer], cv_a[layer], cki_a[layer]
        ar.push()
        KITp = self.alias_hT_bf16([128, PAST])
        pti = ar.i32([128, NPAGES])
        ptf = ar.f32([128, NPAGES])
        idx = ar.i32([128, NPAGES])
        self.dma(pti, ptab.ap().partition_broadcast(128), [], ["pti"])
        self.copy(DVE, ptf, pti, ["pti"], ["ptf"])
        self.ts(DVE, ptf, ptf, float(PAGE), self.iota[:, 0:1], ALU.mult, ALU.add, ["ptf", "iota"], ["ptf"])
        self.copy(DVE, idx, ptf, ["ptf"], ["idx"])

        def gather(dst, table, j, dres):
            return P.op(POOL, lambda e: e.indirect_dma_start(out=dst, out_offset=None, in_=table.ap(),
                                                             in_offset=bass.IndirectOffsetOnAxis(ap=idx[:, j:j + 1], axis=0)),
                        ["idx"], [dres], dma=True)
        KIg = [ar.f32([128, IDXD]) for _ in range(4)]
        KId = [ar.bf16([128, 128]) for _ in range(4)]
        for j in range(NPAGES):
            b4 = j % 4
            gather(KIg[b4], cki_a, j, ("KIg", b4))
            self.copy(DVE, KId[b4][:, 0:64], KIg[b4], [("KIg", b4)], [("KIdA", b4)])
            self.copy(DVE, KId[b4][:, 64:128], KIg[b4], [("KIg", b4)], [("KIdB", b4)])
            self.transpose_to(KITp[:, j * 128:(j + 1) * 128], KId[b4], 128, j, [("KIdA", b4), ("KIdB", b4)], [("KITp", j)])
        kitp = [("KITp", j) for j in range(NPAGES)]
        if SASTOP <= 1:
            ar.pop()
            return
        Zh = ar.bf16([128, 16, 248])
        P.op(POOL, lambda e: e.memset(Zh.rearrange("p a b -> p (a b)"), 0.0), [], ["Zh"])
        for h in range(16):
            hp = (h % 2) * 64
            self.copy(DVE, Zh[hp:hp + 64, h, 120:128], QITs[hp:hp + 64, h // 2, :], ["Zh", "QITs"], ["Zh"])
        wrep = ar.f32([128, 16])
        self.dma(self.scr2_d.ap(), WIsS, ["WIsS"], ["scr2"])
        for sg in range(16):
            self.dma(wrep[sg * 8:(sg + 1) * 8, :], self.scr2_d.ap(), ["scr2"], [("wrep", sg)])
        wr = [("wrep", sg) for sg in range(16)]
        Is = ar.f32([128, 1024])
        Inew = ar.f32([TS, TS])
        Rl = [ar.f32([128, 512]) for _ in range(2)]
        k = 0
        for h in range(16):
            hp = (h % 2) * 64
            for half in range(2):
                bank = 2 * (h % 2) + half
                pb, pr = ps[bank], f"ps{bank}"
                for sg in range(16):
                    self.mm(pb[:, 0:512], Zh[hp:hp + 64, h, 120 - 8 * sg:248 - 8 * sg],
                            KITp[hp:hp + 64, sg * 1024 + half * 512:sg * 1024 + half * 512 + 512], sg == 0, sg == 15, kitp + ["Zh"], [pr])
                rl, rr = Rl[k % 2], ("Rl", k % 2)
                k += 1
                self.act(rl, pb[:, 0:512], ACTF.Relu, [pr], [rr])
                Ic = Is[:, half * 512:(half + 1) * 512]
                if h == 0:
                    self.ts(DVE, Ic, rl, wrep[:, h:h + 1], None, ALU.mult, None, [rr] + wr, [("Is", half)])
                else:
                    self.stt(DVE, Ic, rl, wrep[:, h:h + 1], Ic, ALU.mult, ALU.add, [rr, ("Is", half)] + wr, [("Is", half)])
            self.mm(ps[4][0:TS, 0:TS], QITs[hp:hp + 64, h // 2, :], KITn[hp:hp + 64, :], True, True, ["QITs", "KITn"], ["ps4"])
            self.act(Rl[k % 2][0:TS, 0:TS], ps[4][0:TS, 0:TS], ACTF.Relu, ["ps4"], [("Rl", k % 2)])
            if h == 0:
                self.ts(DVE, Inew, Rl[k % 2][0:TS, 0:TS], WIsS[:, h:h + 1], None, ALU.mult, None, [("Rl", k % 2), "WIsS"], ["Inew"])
            else:
                self.stt(DVE, Inew, Rl[k % 2][0:TS, 0:TS], WIsS[:, h:h + 1], Inew, ALU.mult, ALU.add, [("Rl", k % 2), "WIsS", "Inew"], ["Inew"])
            k += 1
        self.tt(DVE, Inew, Inew, self.cmask[0:TS, 0:TS], ALU.add, ["Inew", "cmask"], ["Inew"])
        if SASTOP <= 2:
            ar.pop()
            return
        Isr = [("Is", 0), ("Is", 1)]
        cand = ar.f32([128, 256])
        Wk = ar.f32([128, 1024])
        src, sres = Is, Isr
        for r in range(32):
            P.op(DVE, lambda e, r=r, src=src: e.max(out=cand[:, r * 8:(r + 1) * 8], in_=src), sres, [("cand", r)])
            if r < 31:
                P.op(DVE, lambda e, r=r, src=src: e.match_replace(out=Wk, in_to_replace=cand[:, r * 8:(r + 1) * 8], in_values=src,
                                                                   imm_value=-1e30), list(sres) + [("cand", r)], ["Wk"])
                src, sres = Wk, ["Wk"]
        self.dma(self.scr_d.ap(), cand, [("cand", r) for r in range(32)], ["scr"])
        C2 = ar.f32([TS, 4096 + TS])
        W2 = ar.f32([TS, 4096 + TS])
        m8 = ar.f32([TS, 32, 8])
        self.dma(C2[:, 0:4096].rearrange("q (s c) -> q s c", s=16), bass.AP(self.scr_d, 0, [[256, TS], [TS * 256, 16], [1, 256]]), ["scr"], ["C2"])
        self.copy(DVE, C2[:, 4096:4096 + TS], Inew, ["Inew", "C2"], ["C2"])
        src, sres = C2, ["C2"]
        for r in range(32):
            P.op(DVE, lambda e, r=r, src=src: e.max(out=m8[:, r, :], in_=src), sres, [("m8s", r)])
            if r < 31:
                P.op(DVE, lambda e, r=r, src=src: e.match_replace(out=W2, in_to_replace=m8[:, r, :], in_values=src, imm_value=-1e30),
                     list(sres) + [("m8s", r)], ["W2"])
                src, sres = W2, ["W2"]
        thr2 = ar.f32([TS, 1])
        thrr = ar.f32([128, 1])
        self.ts(DVE, thr2, m8[:, 31, 7:8], -1e29, None, ALU.max, None, [("m8s", 31)], ["thr2s"])
        self.dma(self.scr3_d.ap(), thr2, ["thr2s"], ["scr3"])
        for sg in range(16):
            self.dma(thrr[sg * 8:(sg + 1) * 8, :], self.scr3_d.ap(), ["scr3"], [("thrr", sg)])
        S01 = ar.bf16([128, 1024])
        S01n = ar.bf16([TS, TS])
        self.ts(DVE, S01, Is, thrr[:, 0:1], None, ALU.is_ge, None, Isr + [("thrr", sg) for sg in range(16)], ["S01s"])
        self.ts(DVE, S01n, Inew, thr2[:, 0:1], None, ALU.is_ge, None, ["Inew", "thr2s"], ["S01n"])
        selTs = ar.bf16([128, 8, 128])
        selTn = ar.bf16([TS, TS])
        for c in range(8):
            self.transpose_to(selTs[:, c, :], S01[:, c * 128:(c + 1) * 128], 128, c, ["S01s"], [("selTs", c)])
        self.transpose_to(selTn, S01n, TS, 0, ["S01n"], ["selTn"])
        sels = [("selTs", c) for c in range(8)] + ["selTn"]
        if SASTOP <= 3:
            ar.pop()
            return
        NNEAR = NPAGES + 1 - JFAR
        Bnear = ar.f32([128, NNEAR, 16, TS])
        Bfar = ar.f32([128, 16, TS])
        Hs = ar.f32([128, 16, TS])
        self.build_table(Bfar, "Bfar", self.LS_d, ZS, 0, 16, PAST - 128 * (JFAR - 1), TS, Hs, "s")
        for j in range(JFAR, NPAGES + 1):
            self.build_table(Bnear[:, j - JFAR], ("Bnear", j), self.LS_d, ZS, 0, 16, PAST - 128 * j, TS, Hs, "s")
        Kg = [ar.f32([128, 512]) for _ in range(2)]
        Vg = [ar.f32([128, 512]) for _ in range(2)]
        Kb = [ar.bf16([128, 512]) for _ in range(2)]
        Vp = [ar.bf16([128, 512]) for _ in range(2)]
        KTp = [ar.bf16([128, 4, 128]) for _ in range(2)]
        Tm = [ar.f32([128, 128]) for _ in range(2)]
        Pe = [ar.bf16([128, 128]) for _ in range(2)]
        Pm = [ar.bf16([128, 128]) for _ in range(2)]
        r3 = lambda ap: ap.rearrange("p (a b) -> p a b", a=16)
        po, pS = ps[2], ps[3]
        for j in range(NPAGES + 1):
            b2 = j % 2
            if j < NPAGES:
                nk = 128
                gather(Kg[b2], ck_a, j, ("Kg", b2))
                gather(Vg[b2], cv_a, j, ("Vg", b2))
                self.copy(POOL, Kb[b2], Kg[b2], [("Kg", b2)], [("Kb", b2)])
                self.copy(DVE, Vp[b2], Vg[b2], [("Vg", b2)], [("Vp", b2)])
                for g in range(4):
                    self.transpose_to(KTp[b2][:, g, :], Kb[b2][:, g * 128:(g + 1) * 128], 128, g, [("Kb", b2)], [("KTp", b2, g)])
                ktp = lambda g: KTp[b2][:, g, :]
                vp = lambda g: Vp[b2][:, g * 128:(g + 1) * 128]
                kres = [("KTp", b2, g) for g in range(4)]
                vres = [("Vp", b2)]
                bias = Bfar if j < JFAR else Bnear[:, j - JFAR]
                bres = ["Bfar"] if j < JFAR else [("Bnear", j)]
                mask = selTs[:, j % 8, (j // 8) * 8:(j // 8) * 8 + 8].unsqueeze(1).to_broadcast([128, 16, TS])
            else:
                nk = TS
                ktp = lambda g: KTo[:, g, NT:TOK]
                vp = lambda g: Vb[0:TS, 8, g * 128:(g + 1) * 128]
                kres, vres = [], []
                bias = Bnear[0:TS, j - JFAR]
                bres = [("Bnear", j)]
                mask = selTn.unsqueeze(1).to_broadcast([TS, 16, TS])
            if SASTOP <= 5:
                continue
            pl, plr = ps[b2], f"ps{b2}"
            for g in range(4):
                self.mm(pl[0:nk, g * 32:(g + 1) * 32], ktp(g), QT[:, 4 * g:4 * g + 4, NT:TOK], True, True, kres, [plr])
            self.tt(DVE, r3(Tm[b2][0:nk, :]), r3(pl[0:nk, 0:128]), bias, ALU.add, [plr] + bres, [("Tms", b2)])
            self.act(Pe[b2][0:nk, :], Tm[b2][0:nk, :], ACTF.Exp, [("Tms", b2)], [("Pes", b2)])
            self.tt(POOL, r3(Pm[b2][0:nk, :]), r3(Pe[b2][0:nk, :]), mask, ALU.mult, [("Pes", b2)] + sels, [("Pms", b2)])
            for g in range(4):
                self.mm(po[:, g * 32:(g + 1) * 32], vp(g), Pm[b2][0:nk, g * 32:(g + 1) * 32], j == 0, j == NPAGES, [("Pms", b2)] + vres, ["ps2"])
                self.mm(pS[:, g * 32:(g + 1) * 32], self.onesB[0:nk, :], Pm[b2][0:nk, g * 32:(g + 1) * 32], j == 0, j == NPAGES,
                        [("Pms", b2), "onesB"], ["ps3"])
        if SASTOP <= 5:
            ar.pop()
            return
        rc = ar.f32([128, 128])
        Ob = ar.bf16([128, 128])
        P.op(DVE, lambda e: e.reciprocal(out=rc, in_=pS[:, 0:128]), ["ps3"], ["rcs"])
        self.tt(DVE, Ob, po[:, 0:128], rc, ALU.mult, ["ps2", "rcs"], ["Obs"])
        self.dma(self.OT_d.ap()[:, :, NT:TOK].rearrange("h p t -> p h t"), r3(Ob), ["Obs"], [("OT_d", "s")], nc_ok=True)
        ar.pop()

    def shared_kv(self, hT, w_kv, o_kb, o_vb):
        ar, P, ps = self.ar, self.P, self.ps
        ar.push()
        self.wstream_init(16)
        tmp4 = [ar.f32([128, 128]) for _ in range(4)]
        KTbo = ar.bf16([128, 4, TOK])
        Vbb = ar.bf16([128, NCH, 512])
        slabs = [(w_kv, 0, 16, c * 128, 128) for c in range(8)]
        self.wstream_set(slabs)
        hreads = [("hT", kc) for kc in range(16)]
        k4 = 0
        for c in range(8):
            Wb, wres = self.wget(c)
            if c < 4:
                for ti, (t0, tn) in enumerate(TT):
                    pb, pr = ps[3 + ti], f"ps{3 + ti}"
                    for kc in range(16):
                        self.mm(pb[:, 0:tn], Wb[:, kc, :], hT[:, kc, t0:t0 + tn], kc == 0, kc == 15, hreads + [wres], [pr])
                    self.act(KTbo[:, c, t0:t0 + tn], pb[:, 0:tn], ACTF.Copy, [pr], [("KTbo", c, ti)])
            for ch in range(NCH):
                ntk = 128 if ch < 8 else TS
                pb, pr = ps[ch % 3], f"ps{ch % 3}"
                for kc in range(16):
                    self.mm(pb[0:ntk, 0:128], hT[:, kc, ch * 128:ch * 128 + ntk], Wb[:, kc, :], kc == 0, kc == 15, hreads + [wres], [pr])
                t4, r4 = tmp4[k4 % 4], ("tmp4", k4 % 4)
                k4 += 1
                self.copy(ACT, t4[0:ntk, :], pb[0:ntk, 0:128], [pr], [r4])
                j = c % 4
                od = o_kb if c < 4 else o_vb
                self.dma(od.ap()[ch * 128:ch * 128 + ntk, j * 128:(j + 1) * 128], t4[0:ntk, :], [r4], [("okvb", c, ch)])
                if c >= 4:
                    self.copy(DVE, Vbb[0:ntk, ch, j * 128:(j + 1) * 128], t4[0:ntk, :], [r4], [("Vbb", ch, j)])
        P.barrier()
        snd, rcv = self.snd_b, self.rcv_b
        parts = []
        for g in range(4):
            self.dma(snd.ap()[g * 128:(g + 1) * 128, :], KTbo[:, g, 0:NT], [], [("snd_b", "k", g)], eng=POOL)
            parts.append(("snd_b", "k", g))
        vview = snd.ap()[512:1024, :].rearrange("r (two c) -> (r two) c", two=2)
        for ch in range(8):
            self.dma(vview[ch * 128:(ch + 1) * 128, :], Vbb[:, ch, :], [], [("snd_b", "v", ch)], eng=POOL)
            parts.append(("snd_b", "v", ch))
        if "cx" not in KSKIP:
            P.op(POOL, lambda e: e.collective_compute("AllGather", ALU.bypass, replica_groups=self.PAIRS,
                                                      ins=[snd.ap().opt()], outs=[rcv.ap().opt()]), parts, ["rcv_b"])
        self.copy(DVE, self.KTbS, KTbo[:, :, NT:TOK], [], ["KTbS"])
        self.copy(DVE, self.VbS, Vbb[0:TS, 8, :], [], ["VbS"])
        ar.pop()
        P.barrier()

    def b_layer(self, bl, hT, w_q):
        ar, P, ps = self.ar, self.P, self.ps
        _, _, _, ck_b, cv_b, _ = self.cache
        snd, rcv = self.snd_b, self.rcv_b
        ar.push()
        KTall = ar.bf16([128, 4, 2048])
        Vall = ar.bf16([128, 16, 512])
        QTsB = ar.bf16([128, 48, TS])
        rv = rcv.ap()[512:1024, :].rearrange("r (two c) -> (r two) c", two=2)
        vview = snd.ap()[512:1024, :].rearrange("r (two c) -> (r two) c", two=2)
        for g in range(4):
            self.dma(KTall[:, g, 0:NT], rcv.ap()[g * 128:(g + 1) * 128, :], [], [("KTall", g, 0)])
            self.dma(KTall[:, g, NT:2 * NT], snd.ap()[g * 128:(g + 1) * 128, :], [], [("KTall", g, 1)])
        for ch in range(8):
            self.dma(Vall[:, ch, :], rv[ch * 128:(ch + 1) * 128, :], [], [("Vall", ch)])
            self.dma(Vall[:, 8 + ch, :], vview[ch * 128:(ch + 1) * 128, :], [], [("Vall", 8 + ch)])
        kv_res = [("KTall", g, a) for g in range(4) for a in range(2)] + [("Vall", c) for c in range(16)]
        ar.push()
        self.wstream_init(16, nstage=2, nbf=2)
        QTg = ar.bf16([128, 12, TOK])
        WT = (256, 640, 2048)
        TBs = [ar.f32([128, 4, w]) for w in WT]
        Hst = ar.f32([128, 4, 512])
        bufs, rcb, Ob = self.attn_bufs()
        hreads = [("hT", kc) for kc in range(16)]
        k = 0
        gi = 0
        for g in range(4):
            slabs = [(w_q, bl * D, 16, grp * 2048 + (4 * g + r) * 128, 128) for grp in range(3) for r in range(4)]
            self.wstream_set(slabs)
            for si in range(12):
                Wb, wres = self.wget(si, depth=1)
                for ti, (t0, tn) in enumerate(TT):
                    pb, pr = ps[7], "ps7"
                    for kc in range(16):
                        self.mm(pb[:, 0:tn], Wb[:, kc, :], hT[:, kc, t0:t0 + tn], kc == 0, kc == 15, hreads + [wres], [pr])
                    self.act(QTg[:, si, t0:t0 + tn], pb[:, 0:tn], ACTF.Copy, [pr], [("QTg", si)], scale=HD ** -0.5)
                grp, r = si // 4, si % 4
                self.copy(DVE, QTsB[:, grp * 16 + 4 * g + r, :], QTg[:, si, NT:TOK], [("QTg", si)], [("QTsB", grp * 16 + 4 * g + r)])
            qres = [("QTg", si) for si in range(12)]
            for grp in range(3):
                self.build_table(TBs[grp], ("TBb", grp), self.LB_d, 3 * ZB, 4 * g, 4, grp * ZB, WT[grp], Hst, "b")
            tiles = []
            for i in range(8):
                tl = []
                for grp in range(3):
                    noff = WT[grp] // 128
                    for off in range(noff):
                        kc = 8 + i - off
                        if kc >= 0:
                            tl.append((grp, off, kc))
                bank_o = 2 + gi % 2
                dst = self.OT_d.ap()[4 * g:4 * g + 4, :, i * 128:(i + 1) * 128].rearrange("h p t -> p h t")
                fin = (lambda gi=gi, bank_o=bank_o, dst=dst: self.attn_finish(gi, (4, 128), bank_o, rcb, Ob, dst))
                for t, (grp, off, kc) in enumerate(tl):
                    tiles.append(dict(first=t == 0, last=t == len(tl) - 1, lhsT_k=KTall[:, g, kc * 128:(kc + 1) * 128],
                                      rhs_q=QTg[:, grp * 4:(grp + 1) * 4, i * 128:(i + 1) * 128], shp=(4, 128),
                                      bias=TBs[grp][:, :, off * 128:(off + 1) * 128],
                                      mask=self.maskB.unsqueeze(1).to_broadcast([128, 4, 128]) if kc < 8 else None,
                                      v_lhsT=Vall[:, kc, g * 128:(g + 1) * 128], nkeys=128, bufs=bufs, bank_o=bank_o,
                                      reads=[("TBb", grp)] + qres + kv_res, finish=fin))
                gi += 1
            k = self.attn_run(tiles, k)
        ar.pop()
        P.barrier()
        ar.push()
        act_list = []
        for kc in range(17):
            for grp, lo in ((0, 15), (1, 12), (2, 0)):
                if kc >= lo:
                    act_list.append((kc, grp))
        BSB = ar.f32([128, len(act_list), 16, TS])
        Hs = ar.f32([128, 16, TS])
        for ti, (kc, grp) in enumerate(act_list):
            self.build_table(BSB[:, ti], ("BSB", ti), self.LB_d, 3 * ZB, 0, 16, grp * ZB + WBUF - 128 * kc, TS, Hs, "sb")
        Kg = [ar.f32([128, 512]) for _ in range(2)]
        Vg = [ar.f32([128, 512]) for _ in range(2)]
        Kb = [ar.bf16([128, 512]) for _ in range(2)]
        Vp = [ar.bf16([128, 512]) for _ in range(2)]
        KTp = [ar.bf16([128, 4, 128]) for _ in range(2)]
        Tm = [ar.f32([128, 128]) for _ in range(2)]
        Pe = [ar.bf16([128, 128]) for _ in range(2)]
        r3 = lambda ap: ap.rearrange("p (a b) -> p a b", a=16)
        po, pS = ps[2], ps[3]
        n_t = len(act_list)
        tix = 0
        for kc in range(17):
            b2 = kc % 2
            if kc < 16:
                nk = 128
                self.dma(Kg[b2], ck_b.ap()[kc * 128:(kc + 1) * 128, :], [], [("Kg", b2)])
                self.dma(Vg[b2], cv_b.ap()[kc * 128:(kc + 1) * 128, :], [], [("Vg", b2)])
                self.copy(POOL, Kb[b2], Kg[b2], [("Kg", b2)], [("Kb", b2)])
                self.copy(DVE, Vp[b2], Vg[b2], [("Vg", b2)], [("Vp", b2)])
                for g in range(4):
                    self.transpose_to(KTp[b2][:, g, :], Kb[b2][:, g * 128:(g + 1) * 128], 128, g, [("Kb", b2)], [("KTp", b2, g)])
                ktp = lambda g: KTp[b2][:, g, :]
                vp = lambda g: Vp[b2][:, g * 128:(g + 1) * 128]
                kres = [("KTp", b2, g) for g in range(4)]
                vres = [("Vp", b2)]
            else:
                nk = TS
                ktp = lambda g: self.KTbS[:, g, :]
                vp = lambda g: self.VbS[:, g * 128:(g + 1) * 128]
                kres, vres = ["KTbS"], ["VbS"]
            for grp in range(3):
                if (kc, grp) not in act_list:
                    continue
                ti = act_list.index((kc, grp))
                tb = tix % 2
                pl, plr = ps[tb], f"ps{tb}"
                for g in range(4):
                    self.mm(pl[0:nk, g * 32:(g + 1) * 32], ktp(g), QTsB[:, grp * 16 + 4 * g:grp * 16 + 4 * g + 4, :], True, True, kres, [plr])
                self.tt(DVE, r3(Tm[tb][0:nk, :]), r3(pl[0:nk, 0:128]), BSB[0:nk, ti], ALU.add, [plr, ("BSB", ti)], [("Tms", tb)])
                self.act(Pe[tb][0:nk, :], Tm[tb][0:nk, :], ACTF.Exp, [("Tms", tb)], [("Pes", tb)])
                for g in range(4):
                    self.mm(po[:, g * 32:(g + 1) * 32], vp(g), Pe[tb][0:nk, g * 32:(g + 1) * 32], tix == 0, tix == n_t - 1, [("Pes", tb)] + vres, ["ps2"])
                    self.mm(pS[:, g * 32:(g + 1) * 32], self.onesB[0:nk, :], Pe[tb][0:nk, g * 32:(g + 1) * 32], tix == 0, tix == n_t - 1,
                            [("Pes", tb), "onesB"], ["ps3"])
                tix += 1
        rc = ar.f32([128, 128])
        Ob = ar.bf16([128, 128])
        P.op(DVE, lambda e: e.reciprocal(out=rc, in_=pS[:, 0:128]), ["ps3"], ["rcs"])
        self.tt(DVE, Ob, po[:, 0:128], rc, ALU.mult, ["ps2", "rcs"], ["Obs"])
        self.dma(self.OT_d.ap()[:, :, NT:TOK].rearrange("h p t -> p h t"), r3(Ob), ["Obs"], [("OT_d", "s")], nc_ok=True)
        ar.pop()
        ar.pop()
        P.barrier()

    def proj_residual(self, w, row0):
        ar, P, ps = self.ar, self.P, self.ps
        h32_d, xpre_d = self.h32_d, self.xpre_d
        ar.push()
        OTs = ar.bf16([128, 16, TOK])
        for h in range(16):
            self.dma(OTs[:, h, :], self.OT_d.ap()[h], [], [("OTs", h)])
        oreads = [("OTs", h) for h in range(16)]
        self.wstream_init(16, nstage=2, nbf=2)
        hs = [ar.f32([128, TOK]) for _ in range(2)]
        slabs = [(w, row0, 16, cc * 128, 128) for cc in range(16)]
        self.wstream_set(slabs)
        for cc in range(16):
            Wd, rd = self.wget(cc, depth=1)
            hx = hs[cc % 2]
            self.dma(hx, h32_d.ap()[cc], [("h32", cc)], [("hs", cc % 2)])
            for ti, (t0, tn) in enumerate(TT):
                pb, pr = ps[ti + 3 * (cc % 2)], f"ps{ti + 3 * (cc % 2)}"
                for h in range(16):
                    self.mm(pb[:, 0:tn], Wd[:, h, :], OTs[:, h, t0:t0 + tn], h == 0, h == 15, oreads + [rd], [pr])
                self.stt(DVE, hx[:, t0:t0 + tn], hx[:, t0:t0 + tn], ALPHA, pb[:, 0:tn], ALU.mult, ALU.add, [("hs", cc % 2), pr], [("hs", cc % 2)])
            self.dma(xpre_d.ap()[cc], hx, [("hs", cc % 2)], [("xpre", cc)])
        ar.pop()
        P.barrier()

    def ffn(self, layer, hT, w_up, cw, cb, w_dn, stT, h32_d, xpre_d, o_ffp, o_ffs, snd_g, rcv_g, hfl):
        ar, P, ps, nc = self.ar, self.P, self.ps, self.nc
        ar.push()
        actT = ar.bf16([128, NFC, TOK])
        cwt = ar.f32([128, 3, NFC])
        cbt = ar.f32([128, NFC])
        stt_ = ar.f32([128, NFC, 2])
        GL = ar.f32([128, NFC, 2])
        GS = ar.f32([128, NFC, 2])
        G01 = ar.f32([128, NFC, 2])
        U01 = ar.f32([128, NFC, 2])
        HAL = ar.f32([128, NFC, 2])
        self.dma(cwt, cw.ap()[layer].rearrange("j (c p) -> p j c", p=128), [], ["cwt"], nc_ok=True)
        self.dma(cbt, cb.ap()[layer].rearrange("(c p) -> p c", p=128), [], ["cbt"], nc_ok=True)
        self.dma(stt_, stT.ap()[layer].rearrange("(c p) j -> p c j", p=128), [], ["stt"], nc_ok=True)
        ar.push()
        self.wstream_init(16)
        GXs = [ar.f32([128, 2 + NT + 2 + TS]) for _ in range(2)]
        CVs = [ar.f32([128, TOK]) for _ in range(2)]
        SGs = [ar.f32([128, TOK]) for _ in range(2)]
        for i in range(2):
            P.op(DVE, lambda e, i=i: e.memset(GXs[i][:, 0:2], 0.0), [], [("GX", i)])
        slabs = []
        for fc in range(NFC):
            slabs.append((w_up, layer * D, 16, fc * 128, 128))
            slabs.append((w_up, layer * D, 16, DFF + fc * 128, 128))
        self.wstream_set(slabs)
        hreads = [("hT", kc) for kc in range(16)]
        for fc in range(NFC):
            Wg, rg = self.wget(2 * fc)
            Wu, ru = self.wget(2 * fc + 1)
            GX = GXs[fc % 2]
            CV = CVs[fc % 2]
            SG = SGs[fc % 2]
            gxr, cvr, sgr = ("GX", fc % 2), ("CV", fc % 2), ("SG", fc % 2)
            self.copy(POOL, GX[:, 2 + NT:2 + NT + 2], stt_[:, fc, :], ["stt"], [gxr])
            par = fc % 2
            PG = [(ps[0][:, 0:512], "ps0"), (ps[1][:, 0:512], "ps1"), (ps[6 + par][:, 0:TS], ("psg", par))]
            PU = [(ps[2 + 2 * par][:, 0:512], f"ps{2 + 2 * par}"), (ps[3 + 2 * par][:, 0:512], f"ps{3 + 2 * par}"),
                  (ps[6 + par][:, 16:16 + TS], ("psu", par))]
            for ti, (t0, tn) in enumerate(TT):
                (pg, rpg), (pu, rpu) = PG[ti], PU[ti]
                for kc in range(16):
                    self.mm(pg, Wg[:, kc, :], hT[:, kc, t0:t0 + tn], kc == 0, kc == 15, hreads + [rg], [rpg])
                for kc in range(16):
                    self.mm(pu, Wu[:, kc, :], hT[:, kc, t0:t0 + tn], kc == 0, kc == 15, hreads + [ru], [rpu])
                g0 = 2 + t0 if ti < 2 else 2 + NT + 2
                self.copy(ACT, GX[:, g0:g0 + tn], pg, [rpg], [gxr])
            for (c0, n, g0) in ((0, NT, 0), (NT, TS, 2 + NT)):
                self.ts(DVE, CV[:, c0:c0 + n], GX[:, g0:g0 + n], cwt[:, 0, fc:fc + 1], cbt[:, fc:fc + 1], ALU.mult, ALU.add,
                        [gxr, "cwt", "cbt"], [cvr])
                self.stt(DVE, CV[:, c0:c0 + n], GX[:, g0 + 1:g0 + 1 + n], cwt[:, 1, fc:fc + 1], CV[:, c0:c0 + n], ALU.mult, ALU.add,
                         [gxr, "cwt", cvr], [cvr])
                self.stt(DVE, CV[:, c0:c0 + n], GX[:, g0 + 2:g0 + 2 + n], cwt[:, 2, fc:fc + 1], CV[:, c0:c0 + n], ALU.mult, ALU.add,
                         [gxr, "cwt", cvr], [cvr])
            self.act(SG, CV, ACTF.Silu, [cvr], [sgr])
            for ti, (t0, tn) in enumerate(TT):
                self.tt(DVE, actT[:, fc, t0:t0 + tn], SG[:, t0:t0 + tn], PU[ti][0], ALU.mult, [sgr, PU[ti][1]], [("actT", fc)])
            self.copy(POOL, GL[:, fc, :], GX[:, 2 + NT - 2:2 + NT], [gxr], [("GL", fc)])
            self.copy(POOL, GS[:, fc, :], GX[:, 2 + NT + 2 + TS - 2:2 + NT + 2 + TS], [gxr], [("GS", fc)])
            self.copy(POOL, G01[:, fc, :], GX[:, 2:4], [gxr], [("G01", fc)])
            self.copy(ACT, U01[:, fc, :], PU[0][0][:, 0:2], [PU[0][1]], [("U01", fc)])
        ar.pop()
        allfc = lambda n: [(n, fc) for fc in range(NFC)]
        self.dma(o_ffp.ap()[layer], GL.rearrange("p a b -> p (a b)"), allfc("GL"), [("o_ffp", layer)])
        self.dma(o_ffs.ap()[layer], GS.rearrange("p a b -> p (a b)"), allfc("GS"), [("o_ffs", layer)])
        self.dma(snd_g.ap(), GL.rearrange("p a b -> p (a b)"), allfc("GL"), ["snd_g"], eng=POOL)
        P.op(POOL, lambda e: e.collective_compute("AllGather", ALU.bypass, replica_groups=self.PAIRS,
                                                  ins=[snd_g.ap().opt()], outs=[rcv_g.ap().opt()]), ["snd_g"], ["rcv_g"])
        self.dma(HAL.rearrange("p a b -> p (a b)"), rcv_g.ap()[0:128, :], ["rcv_g"], ["HAL"], eng=POOL)
        FX = ar.f32([128, NFC, 4])
        CF = ar.f32([128, NFC, 2])
        self.ts(DVE, FX[:, :, 0:2], HAL, hfl[:, 0:1], None, ALU.mult, None, ["HAL", "hfl"], ["FX"])
        self.copy(DVE, FX[:, :, 2:4], G01, allfc("G01") + ["FX"], ["FX"])
        for t in range(2):
            self.tt(DVE, CF[:, :, t], FX[:, :, t], cwt[:, 0, :], ALU.mult, ["FX", "cwt"], ["CF"])
            self.tt(DVE, CF[:, :, t], CF[:, :, t], cbt, ALU.add, ["CF", "cbt"], ["CF"])
            for j in (1, 2):
                self.tt(DVE, FX[:, :, t] if False else HAL[:, :, 0], FX[:, :, t + j], cwt[:, j, :], ALU.mult, ["FX", "cwt", "HAL", "CF"], ["HAL"])
                self.tt(DVE, CF[:, :, t], CF[:, :, t], HAL[:, :, 0], ALU.add, ["CF", "HAL"], ["CF"])
        self.act(CF, CF, ACTF.Silu, ["CF"], ["CF"])
        self.tt(DVE, CF, CF, U01, ALU.mult, ["CF"] + allfc("U01"), ["CF"])
        self.copy(DVE, actT[:, :, 0:2], CF, ["CF"] + allfc("actT"), allfc("actT"))
        ar.push()
        self.wstream_init(22, nstage=3, nbf=3)
        hs = [ar.f32([128, TOK]) for _ in range(2)]
        FH = [(0, 22), (22, 21)]
        slabs = [(w_dn, layer * DFF + f0 * 128, fn, cc * 128, 128) for cc in range(16) for (f0, fn) in FH]
        self.wstream_set(slabs)
        areads = allfc("actT")
        for cc in range(16):
            Wds = [self.wget(2 * cc, depth=2), self.wget(2 * cc + 1, depth=2)]
            hx = hs[cc % 2]
            self.dma(hx, h32_d.ap()[cc], [("h32", cc)], [("hs", cc % 2)])
            for ti, (t0, tn) in enumerate(TT):
                pb, pr = ps[ti + 3 * (cc % 2)], f"ps{ti + 3 * (cc % 2)}"
                for hi, (f0, fn) in enumerate(FH):
                    Wd, rd = Wds[hi]
                    for fl in range(fn):
                        fc = f0 + fl
                        self.mm(pb[:, 0:tn], Wd[:, fl, :], actT[:, fc, t0:t0 + tn], fc == 0, fc == NFC - 1, areads + [rd], [pr])
                self.stt(DVE, hx[:, t0:t0 + tn], hx[:, t0:t0 + tn], ALPHA, pb[:, 0:tn], ALU.mult, ALU.add, [("hs", cc % 2), pr], [("hs", cc % 2)])
            self.dma(xpre_d.ap()[cc], hx, [("hs", cc % 2)], [("xpre", cc)])
        ar.pop()
        ar.pop()
        P.barrier()


_NC = None


def _get_nc():
    global _NC
    if _NC is None:
        _NC = B().build()
    return _NC


def kernel(**inp):
    nc = _get_nc()
    f = lambda a: np.ascontiguousarray(np.asarray(a, dtype=np.float32))
    xp, xs = np.asarray(inp["x_prompt"]), np.asarray(inp["x_sample"])
    shared = {
        "a_w_in": f(inp["a_w_in"]).reshape(NA * D, A_IN),
        "a_w_o": f(inp["a_w_o"]).reshape(NA * D, D),
        "a_kn_g": f(inp["a_kn_g"]), "a_kn_b": f(inp["a_kn_b"]),
        "b_w_kv": f(inp["b_w_kv"]),
        "b_w_q": f(inp["b_w_q"]).reshape(2 * D, 6144),
        "b_w_o": f(inp["b_w_o"]).reshape(2 * D, D),
        "ffn_w_up": f(inp["ffn_w_up"]).reshape(DEPTH * D, 2 * DFF),
        "ffn_conv_w": f(inp["ffn_conv_w"]), "ffn_conv_b": f(inp["ffn_conv_b"]),
        "ffn_w_down": f(inp["ffn_w_down"]).reshape(DEPTH * DFF, D),
        "ln_g": f(inp["ln_g"]).reshape(DEPTH * 2, D), "ln_b": f(inp["ln_b"]).reshape(DEPTH * 2, D),
        "rel_bias": f(inp["rel_bias"]),
    }
    for a in range(NA):
        shared[f"cache_k_a{a}"] = f(inp["cache_k_a"][a]).reshape(NPHYS * PAGE, 512)
        shared[f"cache_v_a{a}"] = f(inp["cache_v_a"][a]).reshape(NPHYS * PAGE, 512)
        shared[f"cache_kidx_a{a}"] = f(inp["cache_kidx_a"][a]).reshape(NPHYS * PAGE, IDXD)
    shared.update(_consts())
    ckb = f(inp["cache_k_b"]).reshape(8, WBUF, 512)
    cvb = f(inp["cache_v_b"]).reshape(8, WBUF, 512)
    pt = np.ascontiguousarray(np.asarray(inp["page_table"], dtype=np.int32))
    in_maps = []
    for c in range(8):
        b, half = c // 2, c % 2
        xT = np.concatenate([xp[b, half * NT:(half + 1) * NT].T, xs[c].T], axis=1)
        m = dict(shared)
        m["xT"] = f(xT)
        m["stT"] = f(np.transpose(np.asarray(inp["state_ffn"])[:, c], (0, 2, 1)))
        m["hflag"] = np.full((128, 1), float(half), np.float32)
        m["pmask"] = np.full((128, 1), 0.0 if half else -1e30, np.float32)
        m["ptab"] = pt[c:c + 1]
        m["ckb"] = ckb[c]
        m["cvb"] = cvb[c]
        in_maps.append(m)
    used = set(t for t in _input_names(nc))
    if "sa" in KSKIP:
        used = set(u for u in used if not u.startswith("cache_"))
    in_maps = [{k: v for k, v in m.items() if k in used} for m in in_maps]
    res = run_bass_kernel_spmd(nc, in_maps, core_ids=list(range(8))).results
    y_p = np.zeros((4, 2048, D), np.float32)
    y_s = np.zeros((8, 8, D), np.float32)
    k_p = np.zeros((NA, 4, 2048, 4, 128), np.float32); v_p = np.zeros_like(k_p)
    ki_p = np.zeros((NA, 4, 2048, IDXD), np.float32)
    k_s = np.zeros((NA, 8, 8, 4, 128), np.float32); v_s = np.zeros_like(k_s)
    ki_s = np.zeros((NA, 8, 8, IDXD), np.float32)
    kb_p = np.zeros((4, 2048, 4, 128), np.float32); vb_p = np.zeros_like(kb_p)
    kb_s = np.zeros((8, 8, 4, 128), np.float32); vb_s = np.zeros_like(kb_s)
    ff_p = np.zeros((DEPTH, 4, 2, DFF), np.float32)
    ff_s = np.zeros((DEPTH, 8, 2, DFF), np.float32)
    for c in range(8):
        r = res[c]
        b, half = c // 2, c % 2
        sl = slice(half * NT, (half + 1) * NT)
        y_p[b, sl] = r["yT"][:, :NT].T
        y_s[c] = r["yT"][:, NT:].T
        k_p[:, b, sl] = r["o_k"][:, :NT].reshape(NA, NT, 4, 128)
        v_p[:, b, sl] = r["o_v"][:, :NT].reshape(NA, NT, 4, 128)
        ki_p[:, b, sl] = r["o_ki"][:, :NT]
        k_s[:, c] = r["o_k"][:, NT:].reshape(NA, TS, 4, 128)
        v_s[:, c] = r["o_v"][:, NT:].reshape(NA, TS, 4, 128)
        ki_s[:, c] = r["o_ki"][:, NT:]
        kb_p[b, sl] = r["o_kb"][:NT].reshape(NT, 4, 128)
        vb_p[b, sl] = r["o_vb"][:NT].reshape(NT, 4, 128)
        kb_s[c] = r["o_kb"][NT:].reshape(TS, 4, 128)
        vb_s[c] = r["o_vb"][NT:].reshape(TS, 4, 128)
        fs = r["o_ffs"].reshape(DEPTH, 128, NFC, 2).transpose(0, 3, 2, 1).reshape(DEPTH, 2, DFF)
        ff_s[:, c] = fs
        if half == 1:
            fp = r["o_ffp"].reshape(DEPTH, 128, NFC, 2).transpose(0, 3, 2, 1).reshape(DEPTH, 2, DFF)
            ff_p[:, b] = fp
    return (y_p, y_s, k_p, v_p, ki_p, k_s, v_s, ki_s, kb_p, vb_p, kb_s, vb_s, ff_p, ff_s)


def _input_names(nc):
    return _INPUTS


_INPUTS = ["xT", "stT", "a_w_in", "a_w_o", "a_kn_g", "a_kn_b", "b_w_kv", "b_w_q", "b_w_o", "ffn_w_up", "ffn_conv_w",
           "ffn_conv_b", "ffn_w_down", "ln_g", "ln_b", "hflag", "pmask", "rel_bias", "ohA", "ohB", "ohS", "Jm", "ident", "cmask",
           "iota", "ptab", "cache_k_a0", "cache_v_a0", "cache_kidx_a0", "cache_k_a1", "cache_v_a1", "cache_kidx_a1", "ckb", "cvb"]
```
